# Optimizing a Trainium2 kernel written in Bass

```python
import jax
import jax.numpy as jnp
from jax import lax
import numpy as np

D_MODEL = 1024
BATCH = 4
SEQ = 4096
DEPTH = 4

HEAD_DIM = 64
RWKV_WIDTH = D_MODEL // 4
RWKV_HEADS = RWKV_WIDTH // HEAD_DIM
RWKV_DECAY_LORA = 64
RWKV_ICLR_LORA = 64
RWKV_GATE_LORA = 128
RWKV_COLS = 3 * RWKV_WIDTH + RWKV_GATE_LORA + 2 * RWKV_DECAY_LORA + 2 * RWKV_ICLR_LORA
MLA_V = 64
MLA_WIDTH = D_MODEL // 2
MLA_HEADS = MLA_WIDTH // MLA_V
MLA_NOPE = 64
MLA_ROPE = 32
MLA_Q_RANK = 384
MLA_KV_RANK = 256
MLA_COLS = MLA_Q_RANK + MLA_KV_RANK + MLA_ROPE
MLA_BLOCK = 128
RET_WIDTH = D_MODEL - RWKV_WIDTH - MLA_WIDTH
RET_HEADS = RET_WIDTH // HEAD_DIM
RET_COLS = 4 * RET_WIDTH
RET_CHUNK = 128

MIX_WIDTH = RWKV_WIDTH + MLA_WIDTH + RET_WIDTH
IN_COLS = RWKV_COLS + MLA_COLS + RET_COLS
D_FF = 2816
N_MOD = 9
ROPE_BASE = 10000.0
NORM_EPS = 1e-6
RWKV_LN_EPS = 64e-5
RET_LN_EPS = 1e-5

kernel_name = 'hybrid_rwkv7_mla_retention_encoder'


def split_cols(t, sizes):
    idx = np.cumsum(sizes)[:-1].tolist()
    return jnp.split(t, idx, axis=-1)


def rms_norm(t, eps=NORM_EPS):
    tf = t.astype(jnp.float32)
    return (tf * lax.rsqrt(jnp.mean(tf * tf, axis=-1, keepdims=True) + eps)).astype(t.dtype)


def head_norm(t, eps):
    tf = t.astype(jnp.float32)
    mu = jnp.mean(tf, axis=-1, keepdims=True)
    var = jnp.mean(jnp.square(tf - mu), axis=-1, keepdims=True)
    return (tf - mu) * lax.rsqrt(var + eps)


def modulate(t, shift, scale):
    return t * (1.0 + scale[:, None, :]) + shift[:, None, :]


def swiglu(h, w_in, w_out):
    gate, up = jnp.split(h @ w_in, 2, axis=-1)
    return (jax.nn.silu(gate) * up) @ w_out


def rotary(t, positions):
    d = t.shape[-1]
    inv = ROPE_BASE ** (-jnp.arange(0, d, 2, dtype=jnp.float32) / d)
    ang = positions.astype(jnp.float32)[..., None] * inv
    ang = ang.reshape(ang.shape[:2] + (1,) * (t.ndim - 3) + ang.shape[-1:])
    cos, sin = jnp.cos(ang).astype(t.dtype), jnp.sin(ang).astype(t.dtype)
    t1, t2 = t[..., : d // 2], t[..., d // 2:]
    return jnp.concatenate([t1 * cos - t2 * sin, t2 * cos + t1 * sin], axis=-1)


def centred_shift(t, mu):
    prev = jnp.pad(t[:, :-1], ((0, 0), (1, 0), (0, 0)))
    nxt = jnp.pad(t[:, 1:], ((0, 0), (0, 1), (0, 0)))
    return t + mu * (0.5 * (prev + nxt) - t)


def _dir_stack(t):
    t = jnp.stack([t[:, :, 0], jnp.flip(t[:, :, 1], axis=1)], axis=2)
    return jnp.transpose(t, (1, 2, 0, 3, 4))


def rwkv7_mixer(z, mu, w0, w_up, a0, a_up, g_up, k_k, k_a, r_k, ln_g, ln_b):
    B, S, _ = z.shape
    H, N = RWKV_HEADS, HEAD_DIM
    z = centred_shift(z, mu)
    r, k, v, g_lo, w_lo, a_lo = split_cols(
        z, [RWKV_WIDTH, RWKV_WIDTH, RWKV_WIDTH, RWKV_GATE_LORA, 2 * RWKV_DECAY_LORA, 2 * RWKV_ICLR_LORA])
    w_lo = w_lo.reshape(B, S, 2, RWKV_DECAY_LORA)
    a_lo = a_lo.reshape(B, S, 2, RWKV_ICLR_LORA)
    w_raw = (w0 + jnp.einsum('bsdr,drc->bsdc', jnp.tanh(w_lo), w_up)).astype(jnp.float32)
    decay = jnp.exp(-jnp.exp(-jax.nn.softplus(-w_raw) - 0.5)).reshape(B, S, 2, H, N)
    a = jax.nn.sigmoid(a0 + jnp.einsum('bsdr,drc->bsdc', a_lo, a_up))
    g = jax.nn.sigmoid(g_lo) @ g_up
    kk = (k * k_k).reshape(B, S, H, N).astype(jnp.float32)
    kk = kk / jnp.maximum(jnp.linalg.norm(kk, axis=-1, keepdims=True), 1e-12)
    kk2 = jnp.broadcast_to(kk[:, :, None], (B, S, 2, H, N))
    a5 = a.reshape(B, S, 2, H, N)
    k_mod = (k[:, :, None, :] * (1.0 + (a - 1.0) * k_a)).reshape(B, S, 2, H, N)
    rh, kh, vh = (t.reshape(B, S, H, N) for t in (r, k, v))
    r2 = jnp.broadcast_to(rh[:, :, None], (B, S, 2, H, N))
    v2 = jnp.broadcast_to(vh[:, :, None], (B, S, 2, H, N))
    xs = tuple(_dir_stack(t) for t in (decay, kk2, kk2 * a5, k_mod, v2, r2))

    def step(state, inp):
        w_t, kk_t, kka_t, k_t, v_t, r_t = inp
        sk = jnp.einsum('dbhvk,dbhk->dbhv', state, kk_t)
        new = (state * w_t[..., None, :] - sk[..., :, None] * kka_t[..., None, :]
               + v_t[..., :, None] * k_t[..., None, :])
        y_fwd = jnp.einsum('bhvk,bhk->bhv', new[0], r_t[0])
        y_bwd = jnp.einsum('bhvk,bhk->bhv', state[1], r_t[1])
        return new, jnp.stack([y_fwd, y_bwd])

    state0 = jnp.zeros((2, B, H, N, N), jnp.float32)
    _, ys = lax.scan(step, state0, xs)
    y = ys[:, 0] + jnp.flip(ys[:, 1], axis=0)
    y = jnp.transpose(y, (1, 0, 2, 3))
    y = (head_norm(y, RWKV_LN_EPS) * ln_g + ln_b).astype(z.dtype)
    bonus = jnp.sum(rh * kh * r_k, axis=-1, keepdims=True) * vh
    return (y + bonus).reshape(B, S, RWKV_WIDTH) * g


def mla_mixer(z, positions, q_norm_g, w_uq, kv_norm_g, w_ukv):
    B, S, _ = z.shape
    H = MLA_HEADS
    cq, ckv, kpe = split_cols(z, [MLA_Q_RANK, MLA_KV_RANK, MLA_ROPE])
    q = ((rms_norm(cq) * q_norm_g) @ w_uq).reshape(B, S, H, MLA_NOPE + MLA_ROPE)
    kv = ((rms_norm(ckv) * kv_norm_g) @ w_ukv).reshape(B, S, H, MLA_NOPE + MLA_V)
    q_nope, q_pe = q[..., :MLA_NOPE], rotary(q[..., MLA_NOPE:], positions)
    k_nope, v = kv[..., :MLA_NOPE], kv[..., MLA_NOPE:]
    k_pe = rotary(kpe, positions)
    scale = (MLA_NOPE + MLA_ROPE) ** -0.5
    nb = S // MLA_BLOCK
    qn_b = jnp.transpose(q_nope.reshape(B, nb, MLA_BLOCK, H, MLA_NOPE), (1, 0, 2, 3, 4))
    qp_b = jnp.transpose(q_pe.reshape(B, nb, MLA_BLOCK, H, MLA_ROPE), (1, 0, 2, 3, 4))

    def attend(blk):
        qn, qp = blk
        s = (jnp.einsum('bqhd,bkhd->bhqk', qn, k_nope)
             + jnp.einsum('bqhr,bkr->bhqk', qp, k_pe))
        p = jax.nn.softmax(s.astype(jnp.float32) * scale, axis=-1).astype(v.dtype)
        return jnp.einsum('bhqk,bkhd->bqhd', p, v)

    o = lax.map(attend, (qn_b, qp_b))
    return jnp.transpose(o, (1, 0, 2, 3, 4)).reshape(B, S, MLA_WIDTH)


def retention_dir(q, k, v, log_gamma, strict):
    B, H, S, d = q.shape
    C = RET_CHUNK
    n = S // C
    qc, kc, vc = (t.reshape(B, H, n, C, d) for t in (q, k, v))
    lg = log_gamma.astype(jnp.float32)
    pos = jnp.arange(C, dtype=jnp.float32)
    diff = pos[:, None] - pos[None, :]
    mask = diff > 0 if strict else diff >= 0
    dmat = jnp.where(mask, jnp.exp(lg[:, None, None] * jnp.maximum(diff, 0.0)), 0.0)
    scores = jnp.einsum('bhncd,bhnmd->bhncm', qc, kc) * dmat[None, :, None]
    inner = jnp.einsum('bhncm,bhnme->bhnce', scores, vc)
    k_w = kc * jnp.exp(lg[:, None] * (C - 1.0 - pos)[None, :])[None, :, None, :, None]
    kv = jnp.einsum('bhncd,bhnce->nbhde', k_w, vc)
    chunk_decay = jnp.exp(lg * C)[None, :, None, None]

    def step(R, kv_n):
        return R * chunk_decay + kv_n, R

    _, r_prev = lax.scan(step, jnp.zeros((B, H, d, d), jnp.float32), kv)
    q_w = qc * jnp.exp(lg[:, None] * (pos + 1.0)[None, :])[None, :, None, :, None]
    cross = jnp.einsum('bhncd,nbhde->bhnce', q_w, r_prev)
    return (inner + cross).reshape(B, H, S, d)


def retention_mixer(z, positions, log_rate, gn_g):
    B, S, _ = z.shape
    H, d = RET_HEADS, HEAD_DIM
    q, k, v, gate = split_cols(z, [RET_WIDTH] * 4)
    q = rotary(q.reshape(B, S, H, d), positions)
    k = rotary(k.reshape(B, S, H, d), positions) * (d ** -0.5)
    v = v.reshape(B, S, H, d)
    q, k, v = (jnp.transpose(t, (0, 2, 1, 3)) for t in (q, k, v))
    log_gamma = -jnp.exp(log_rate.astype(jnp.float32))
    y_f = retention_dir(q, k, v, log_gamma[0], False)
    y_b = jnp.flip(retention_dir(jnp.flip(q, 2), jnp.flip(k, 2), jnp.flip(v, 2), log_gamma[1], True), 2)
    y = jnp.transpose(y_f + y_b, (0, 2, 1, 3))
    y = (head_norm(y, RET_LN_EPS) * gn_g).astype(z.dtype).reshape(B, S, RET_WIDTH)
    return jax.nn.silu(gate) * y


def setup_inputs(seed: int = 0) -> dict:
    key = jax.random.key(seed)
    ks = jax.random.split(key, 32)
    L, D = DEPTH, D_MODEL
    C = RWKV_WIDTH

    def nrm(k, shape, scale):
        return jax.random.normal(k, shape, jnp.float32) * scale

    ret_base = jnp.log(2.0 ** (-5.0 - jnp.arange(RET_HEADS, dtype=jnp.float32)))
    return {
        'x': nrm(ks[0], (BATCH, SEQ, D), 1.0),
        'c': nrm(ks[1], (BATCH, D), 1.0),
        'positions': jnp.tile(jnp.arange(SEQ, dtype=jnp.int32)[None, :], (BATCH, 1)),
        'w_ada': nrm(ks[2], (L, D, N_MOD * D), 0.5 * D ** -0.5),
        'b_ada': nrm(ks[3], (L, N_MOD * D), 0.02),
        'w_ff1_in': nrm(ks[4], (L, D, 2 * D_FF), D ** -0.5),
        'w_ff1_out': nrm(ks[5], (L, D_FF, D), D_FF ** -0.5),
        'w_ff2_in': nrm(ks[6], (L, D, 2 * D_FF), D ** -0.5),
        'w_ff2_out': nrm(ks[7], (L, D_FF, D), D_FF ** -0.5),
        'w_in': nrm(ks[8], (L, D, IN_COLS), D ** -0.5),
        'w_out': nrm(ks[9], (L, MIX_WIDTH, D), MIX_WIDTH ** -0.5),
        'rwkv_mu': jax.random.uniform(ks[10], (L, RWKV_COLS), jnp.float32),
        'rwkv_w0': jax.random.uniform(ks[11], (L, 2, C), jnp.float32, minval=-5.0, maxval=1.0),
        'rwkv_w_up': nrm(ks[12], (L, 2, RWKV_DECAY_LORA, C), 0.5 * RWKV_DECAY_LORA ** -0.5),
        'rwkv_a0': nrm(ks[13], (L, 2, C), 0.5),
        'rwkv_a_up': nrm(ks[14], (L, 2, RWKV_ICLR_LORA, C), 0.5 * RWKV_ICLR_LORA ** -0.5),
        'rwkv_g_up': nrm(ks[15], (L, RWKV_GATE_LORA, C), RWKV_GATE_LORA ** -0.5),
        'rwkv_k_k': 0.85 + nrm(ks[16], (L, C), 0.05),
        'rwkv_k_a': 1.0 + nrm(ks[17], (L, C), 0.05),
        'rwkv_r_k': nrm(ks[18], (L, RWKV_HEADS, HEAD_DIM), 0.1),
        'rwkv_ln_g': 1.0 + nrm(ks[19], (L, RWKV_HEADS, HEAD_DIM), 0.05),
        'rwkv_ln_b': nrm(ks[20], (L, RWKV_HEADS, HEAD_DIM), 0.02),
        'mla_q_norm_g': 1.0 + nrm(ks[21], (L, MLA_Q_RANK), 0.05),
        'mla_w_uq': nrm(ks[22], (L, MLA_Q_RANK, MLA_HEADS * (MLA_NOPE + MLA_ROPE)), MLA_Q_RANK ** -0.5),
        'mla_kv_norm_g': 1.0 + nrm(ks[23], (L, MLA_KV_RANK), 0.05),
        'mla_w_ukv': nrm(ks[24], (L, MLA_KV_RANK, MLA_HEADS * (MLA_NOPE + MLA_V)), MLA_KV_RANK ** -0.5),
        'ret_log_rate': ret_base + nrm(ks[25], (L, 2, RET_HEADS), 0.1),
        'ret_gn_g': 1.0 + nrm(ks[26], (L, RET_HEADS, HEAD_DIM), 0.05),
        'final_norm_g': 1.0 + nrm(ks[27], (D,), 0.05),
    }


def reference(x, c, positions, w_ada, b_ada, w_ff1_in, w_ff1_out, w_ff2_in, w_ff2_out,
              w_in, w_out, rwkv_mu, rwkv_w0, rwkv_w_up, rwkv_a0, rwkv_a_up, rwkv_g_up,
              rwkv_k_k, rwkv_k_a, rwkv_r_k, rwkv_ln_g, rwkv_ln_b, mla_q_norm_g, mla_w_uq,
              mla_kv_norm_g, mla_w_ukv, ret_log_rate, ret_gn_g, final_norm_g):
    cond = jax.nn.silu(c)
    for l in range(DEPTH):
        mod = cond @ w_ada[l] + b_ada[l]
        sh1, sc1, g1, sh2, sc2, g2, sh3, sc3, g3 = jnp.split(mod, N_MOD, axis=-1)
        h = modulate(rms_norm(x), sh1, sc1)
        x = x + 0.5 * g1[:, None, :] * swiglu(h, w_ff1_in[l], w_ff1_out[l])
        h = modulate(rms_norm(x), sh2, sc2)
        z = h @ w_in[l]
        z_a, z_b, z_c = split_cols(z, [RWKV_COLS, MLA_COLS, RET_COLS])
        o_a = rwkv7_mixer(z_a, rwkv_mu[l], rwkv_w0[l], rwkv_w_up[l], rwkv_a0[l], rwkv_a_up[l],
                          rwkv_g_up[l], rwkv_k_k[l], rwkv_k_a[l], rwkv_r_k[l], rwkv_ln_g[l], rwkv_ln_b[l])
        o_b = mla_mixer(z_b, positions, mla_q_norm_g[l], mla_w_uq[l], mla_kv_norm_g[l], mla_w_ukv[l])
        o_c = retention_mixer(z_c, positions, ret_log_rate[l], ret_gn_g[l])
        mixed = jnp.concatenate([o_a, o_b, o_c], axis=-1) @ w_out[l]
        x = x + g2[:, None, :] * mixed
        h = modulate(rms_norm(x), sh3, sc3)
        x = x + 0.5 * g3[:, None, :] * swiglu(h, w_ff2_in[l], w_ff2_out[l])
    return rms_norm(x) * final_norm_g
```

```python
import numpy as np
from contextlib import ExitStack
import concourse.bass as bass
import concourse.mybir as mybir
from concourse.bass_utils import run_bass_kernel_spmd

F32 = mybir.dt.float32
BF16 = mybir.dt.bfloat16
I32 = mybir.dt.int32
AF = mybir.ActivationFunctionType
ALU = mybir.AluOpType
AX = mybir.AxisListType

ENGS = ("tensor", "vector", "scalar", "gpsimd", "sync")
NDMA = 8

T = 4096
D = 1024
DEPTH = 4
DFF = 2816
NF = DFF // 128
NT = 512
EPS = 1e-6


class Buf:
    __slots__ = ("name", "ap", "lw", "rd")

    def __init__(self, ap, name=""):
        self.ap = ap
        self.name = name
        self.lw = None
        self.rd = {}

    def __getitem__(self, idx):
        return self.ap[idx]


class Prog:
    def __init__(self, nc):
        self.nc = nc
        self.ops = {e: [] for e in ENGS}
        self.cnt = {}
        self.seen = {e: {} for e in ENGS}
        self.dma_i = {e: 0 for e in ENGS}
        self.n = 0

    def _deps(self, eng, reads, writes):
        waits = {}

        def need(t):
            if t is None:
                return
            k, v = t
            if waits.get(k, 0) < v:
                waits[k] = v
        for b in reads:
            need(b.lw)
        for b in writes:
            need(b.lw)
            for k, v in b.rd.items():
                need((k, v))
        out = []
        seen = self.seen[eng]
        own = ("c", "tensor") if eng == "tensor" else None
        for k, v in waits.items():
            if k == own:
                continue
            if seen.get(k, 0) < v:
                seen[k] = v
                out.append((k, v))
        return out

    def _mark(self, k, v, reads, writes):
        for b in reads:
            if b.rd.get(k, 0) < v:
                b.rd[k] = v
        for b in writes:
            b.lw = (k, v)
            b.rd = {}

    def op(self, eng, fn, reads=(), writes=()):
        waits = self._deps(eng, reads, writes)
        k = ("c", eng)
        v = self.cnt.get(k, 0) + 1
        self.cnt[k] = v
        self.ops[eng].append((waits, fn, (k, 1)))
        self._mark(k, v, reads, writes)
        self.n += 1

    def dma(self, eng, out_ap, in_ap, reads=(), writes=(), **kw):
        waits = self._deps(eng, reads, writes)
        i = self.dma_i[eng]
        self.dma_i[eng] = i + 1
        k = ("d", eng, i % NDMA)
        prev = self.cnt.get(k, 0)
        if prev and self.seen[eng].get(k, 0) < prev:
            self.seen[eng][k] = prev
            waits.append((k, prev))
        v = prev + 16
        self.cnt[k] = v
        self.ops[eng].append((waits, lambda e: e.dma_start(out=out_ap, in_=in_ap, **kw), (k, 16)))
        self._mark(k, v, reads, writes)
        self.n += 1

    def barrier(self):
        for e in ENGS:
            waits = []
            for k, v in self.cnt.items():
                if self.seen[e].get(k, 0) < v:
                    self.seen[e][k] = v
                    waits.append((k, v))
            self.ops[e].append((waits, None, None))

    def emit(self, stack):
        nc = self.nc
        sems = {}
        for k in self.cnt:
            sems[k] = stack.enter_context(nc.semaphore("s_" + "_".join(str(x) for x in k)))
        block = stack.enter_context(nc.Block())
        ops = self.ops

        def mk(engname):
            def body(e):
                for waits, fn, inc in ops[engname]:
                    for k, v in waits:
                        e.wait_ge(sems[k], v)
                    if fn is not None:
                        fn(e).then_inc(sems[inc[0]], inc[1])
            return body
        block.tensor(mk("tensor"))
        block.vector(mk("vector"))
        block.scalar(mk("scalar"))
        block.gpsimd(mk("gpsimd"))
        block.sync(mk("sync"))


class Arena:
    def __init__(self, ap, nelem):
        self.ap = ap
        self.n = nelem
        self.off = 0

    def reset(self, keep=0):
        self.off = keep

    def alloc(self, shape, name="", parts=128):
        n = int(np.prod(shape))
        assert self.off + n <= self.n, (name, self.off, n, self.n)
        a = self.ap[0:parts, self.off:self.off + n]
        self.off += n
        if len(shape) == 2:
            a = a.rearrange("p (a b) -> p a b", a=shape[0])
        elif len(shape) == 3:
            a = a.rearrange("p (a b c) -> p a b c", a=shape[0], b=shape[1])
        return Buf(a, name)


def build_nc(cfg):
    nl = cfg.get("layers", DEPTH)
    nc = bass.Bass("TRN2", target_bir_lowering=False)
    din = {}

    def inp(name, shape, dt=F32):
        din[name] = nc.dram_tensor(name, list(shape), dt, kind="ExternalInput").ap()
        return din[name]

    xT = inp("xT", [D, T])
    cvec = inp("cvec", [128, 8])
    w_ada = inp("w_ada", [DEPTH, D, 9 * D])
    b_adaT = inp("b_adaT", [DEPTH, 128, 72])
    w_ff_in = [inp("w_ff1_in", [DEPTH, D, 2 * DFF]), inp("w_ff2_in", [DEPTH, D, 2 * DFF])]
    w_ff_out = [inp("w_ff1_out", [DEPTH, DFF, D]), inp("w_ff2_out", [DEPTH, DFF, D])]
    fng = inp("fng", [128, 8])
    pos = inp("pos", [1, T], I32)
    cst = inp("cst", [128, 136])
    w_in = inp("w_in", [DEPTH, D, 2848])
    w_out = inp("w_out", [DEPTH, D, D])
    rwkv_mu = inp("rwkv_mu", [DEPTH, 1152])
    mla_qgT = inp("mla_qgT", [DEPTH, 128, 3])
    mla_kvgT = inp("mla_kvgT", [DEPTH, 128, 2])
    mla_w_uq = inp("mla_w_uq", [DEPTH, 384, 768])
    mla_w_ukv = inp("mla_w_ukv", [DEPTH, 256, 1024])
    basec = inp("basec", [128, 512 + 64])
    ret_lr = inp("ret_lr", [DEPTH, 1, 8])
    ret_gT = inp("ret_gT", [DEPTH, 64, 4])
    rwp = inp("rwp", [DEPTH, 128, 18])
    rw_wup = inp("rw_wup", [DEPTH, 128, 256])
    rw_aup = inp("rw_aup", [DEPTH, 128, 256])
    rw_gup = inp("rw_gup", [DEPTH, 128, 256])
    rmask = inp("rmask", [64, 2560])
    ucst = inp("ucst", [128, 384])
    dk = "ExternalOutput" if cfg.get("debug") else "Internal"
    opsD = nc.dram_tensor("opsD", [2, 256, 64, 392], F32).ap()
    dbgT = nc.dram_tensor("dbgT", [24, 64, 512], F32, kind=dk).ap()
    dbgT_b = Buf(dbgT, "dbgT")
    yD = nc.dram_tensor("yD", [2, 256, T], F32, kind=dk).ap()
    gTd = nc.dram_tensor("gTd", [256, T], F32, kind=dk).ap()
    bonT = nc.dram_tensor("bonT", [256, T], F32, kind=dk).ap()
    opsD_b = Buf(opsD, "opsD"); yD_b = Buf(yD, "yD"); gTd_b = Buf(gTd, "gTd"); bonT_b = Buf(bonT, "bonT")
    NZ = 28
    zT = nc.dram_tensor("zT", [NZ * 128, T], F32).ap()
    oT = nc.dram_tensor("oT", [D, T], BF16).ap()
    tabs = nc.dram_tensor("tabs", [4, 128, T], F32).ap()
    qT = nc.dram_tensor("qT", [8, 96, T], BF16).ap()
    kT = nc.dram_tensor("kT", [8, 96, T], BF16).ap()
    zT_b = Buf(zT, "zT"); oT_b = Buf(oT, "oT"); tabs_b = Buf(tabs, "tabs")
    qT_b = Buf(qT, "qT"); kT_b = Buf(kT, "kT")
    yT = nc.dram_tensor("yT", [D, T], F32, kind="ExternalOutput").ap()
    xs = nc.dram_tensor("xs", [D, T], F32).ap()

    G = 1024
    NG = T // G
    xs_g = [Buf(xs, "xs%d" % g) for g in range(NG)]
    xT_g = [Buf(xT, "xT%d" % g) for g in range(NG)]
    yT_g = [Buf(yT, "yT%d" % g) for g in range(NG)]

    with ExitStack() as st:
        P = Prog(nc)
        NA32 = 18 * 1024
        NA16 = 48 * 1024
        a32 = Arena(st.enter_context(nc.sbuf_tensor("a32", [128, NA32], F32))[:], NA32)
        a16 = Arena(st.enter_context(nc.sbuf_tensor("a16", [128, NA16], BF16))[:], NA16)
        psb = [Buf(st.enter_context(nc.psum_tensor("ps%d" % i, [128, 512], F32))[:], "ps%d" % i)
               for i in range(8)]
        psi = [0]

        def nps():
            psi[0] = (psi[0] + 1) % 6
            return psb[psi[0]]
        ones32 = a32.alloc([128], "ones32")
        cond = a32.alloc([8], "cond")
        modT = a32.alloc([72], "modT")
        sc32 = a32.alloc([24], "sc32")
        hg = a32.alloc([24], "hg")
        fngt = a32.alloc([8], "fng")
        epsc = a32.alloc([4], "epsc")
        keep32 = a32.off
        ones16 = a16.alloc([128], "ones16")
        keep16 = a16.off

        P.op("vector", lambda e: e.memset(ones32.ap, 1.0), writes=[ones32])
        P.op("vector", lambda e: e.memset(ones16.ap, 1.0), writes=[ones16])
        P.op("vector", lambda e: e.memset(epsc.ap, EPS), writes=[epsc])
        P.dma("sync", cond.ap, cvec, writes=[cond])
        P.dma("sync", fngt.ap, fng, writes=[fngt])
        P.op("scalar", lambda e: e.activation(cond.ap, cond.ap, AF.Silu), reads=[cond], writes=[cond])

        def phase_mod(l):
            P.barrier()
            a32.reset(keep32)
            a16.reset(keep16)
            wa = [a32.alloc([8, 512], "wa%d" % i) for i in range(2)]
            acc = [a32.alloc([512], "acc%d" % i) for i in range(2)]
            bt = a32.alloc([72], "bt")
            P.dma("sync", bt.ap, b_adaT[l], writes=[bt])
            pm = psb[7]
            for j in range(18):
                w = wa[j % 2]
                a = acc[j % 2]
                src = w_ada[l, :, j * 512:(j + 1) * 512].rearrange("(kc p) n -> p kc n", p=128)
                P.dma("sync", w.ap, src, writes=[w])
                for kc in range(8):
                    if kc == 0:
                        P.op("vector", lambda e, w=w, a=a, kc=kc: e.tensor_scalar(
                            a.ap, w.ap[:, kc, :], cond.ap[:, kc:kc + 1], None, ALU.mult),
                            reads=[w, cond], writes=[a])
                    else:
                        P.op("vector", lambda e, w=w, a=a, kc=kc: e.scalar_tensor_tensor(
                            a.ap, w.ap[:, kc, :], cond.ap[:, kc:kc + 1], a.ap, ALU.mult, ALU.add),
                            reads=[w, cond, a], writes=[a])
                for jj in range(4):
                    col = j * 4 + jj
                    P.op("tensor", lambda e, a=a, jj=jj, col=col: e.matmul(
                        pm.ap[:, col:col + 1], a.ap[:, jj * 128:(jj + 1) * 128], ones32.ap[:, 0:1],
                        start=True, stop=True), reads=[a, ones32], writes=[pm])
            P.op("vector", lambda e: e.tensor_tensor(modT.ap, pm.ap[:, 0:72], bt.ap, ALU.add),
                 reads=[pm, bt], writes=[modT])
            for i in range(3):
                P.op("vector", lambda e, i=i: e.tensor_scalar(
                    sc32.ap[:, i * 8:(i + 1) * 8], modT.ap[:, (3 * i + 1) * 8:(3 * i + 2) * 8],
                    1.0, 32.0, ALU.add, ALU.mult), reads=[modT], writes=[sc32])
                P.op("vector", lambda e, i=i: e.tensor_scalar(
                    hg.ap[:, i * 8:(i + 1) * 8], modT.ap[:, (3 * i + 2) * 8:(3 * i + 3) * 8],
                    (1.0 if i == 1 else 0.5), None, ALU.mult), reads=[modT], writes=[hg])

        def rms_mod(xt, n, h, ni, tmp, sq, extra_r=()):
            P.op("scalar", lambda e: e.activation(sq.ap[:, :, 0:n], xt.ap[:, :, 0:n], AF.Square),
                 reads=[xt], writes=[sq])
            ps = nps()
            for kc in range(8):
                P.op("tensor", lambda e, kc=kc: e.matmul(ps.ap[:, 0:n], ones16.ap, sq.ap[:, kc, 0:n],
                                                         start=(kc == 0), stop=(kc == 7)),
                     reads=[sq, ones16], writes=[ps])
            rs = tmp
            return ps

        def norm_tile(xt, n, h, ni, tmp, sq, rstd):
            ps = rms_mod(xt, n, h, ni, tmp, sq)
            P.op("scalar", lambda e: e.activation(rstd.ap[:, 0:n], ps.ap[:, 0:n], AF.Sqrt,
                                                  bias=epsc.ap[:, 1:2], scale=1.0),
                 reads=[ps, epsc], writes=[rstd])
            P.op("vector", lambda e: e.reciprocal(rstd.ap[:, 0:n], rstd.ap[:, 0:n]),
                 reads=[rstd], writes=[rstd])
            for kc in range(8):
                P.op("vector", lambda e, kc=kc: e.tensor_tensor(
                    tmp.ap[:, kc, 0:n], xt.ap[:, kc, 0:n], rstd.ap[:, 0:n], ALU.mult),
                    reads=[xt, rstd], writes=[tmp])
            for kc in range(8):
                if ni >= 0:
                    P.op("scalar", lambda e, kc=kc: e.activation(
                        h[:, kc, 0:n] if not isinstance(h, Buf) else h.ap[:, kc, 0:n],
                        tmp.ap[:, kc, 0:n], AF.Identity,
                        bias=modT.ap[:, ni * 24 + kc:ni * 24 + kc + 1],
                        scale=sc32.ap[:, ni * 8 + kc:ni * 8 + kc + 1]),
                        reads=[tmp, modT, sc32], writes=[h])
                else:
                    P.op("scalar", lambda e, kc=kc: e.activation(
                        h.ap[:, kc, 0:n], tmp.ap[:, kc, 0:n], AF.Identity,
                        scale=fngt.ap[:, kc:kc + 1]), reads=[tmp, fngt], writes=[h])

        def phase_ffn(l, which, src, src_g, dst, dst_g):
            ni = 0 if which == 0 else 2
            P.barrier()
            a32.reset(keep32)
            a16.reset(keep16)
            win = w_ff_in[which]
            wout = w_ff_out[which]
            xt = [a32.alloc([8, NT], "xt%d" % i) for i in range(2)]
            tmp = a32.alloc([8, NT], "tmp")
            rstd = a32.alloc([NT], "rstd")
            sg = [a32.alloc([NT], "sg%d" % i) for i in range(2)]
            xr = [a32.alloc([NT], "xr%d" % i) for i in range(3)]
            sq = a16.alloc([8, NT], "sq")
            h = a16.alloc([8, G], "h")
            act = [a16.alloc([G], "act%d" % f) for f in range(NF)]
            wgu = [a16.alloc([8, 256], "wgu%d" % i) for i in range(3)]
            wo = [a16.alloc([NF, 128], "wo%d" % i) for i in range(2)]
            P.op("vector", lambda e: e.memset(epsc.ap[:, 1:2], 1024.0 * EPS), writes=[epsc])
            for g in range(NG):
                t0 = g * G
                hh = [Buf(h.ap[:, :, i * NT:(i + 1) * NT], "h%d" % i) for i in range(G // NT)]
                for tt in range(G // NT):
                    x_ = xt[tt % 2]
                    P.dma("sync", x_.ap, src[:, t0 + tt * NT:t0 + (tt + 1) * NT].rearrange(
                        "(kc p) t -> p kc t", p=128), reads=[src_g[g]], writes=[x_])
                    norm_tile(x_, NT, hh[tt], ni, tmp, sq, rstd)
                for f in range(NF):
                    w = wgu[f % 3]
                    P.dma("gpsimd", w.ap[:, :, 0:128], win[l, :, f * 128:(f + 1) * 128].rearrange(
                        "(kc p) n -> p kc n", p=128), writes=[w])
                    P.dma("gpsimd", w.ap[:, :, 128:256],
                          win[l, :, DFF + f * 128:DFF + (f + 1) * 128].rearrange(
                        "(kc p) n -> p kc n", p=128), writes=[w])
                    for tt in range(G // NT):
                        pg = nps()
                        pu = nps()
                        for kc in range(8):
                            P.op("tensor", lambda e, kc=kc, pg=pg, w=w, tt=tt: e.matmul(
                                pg.ap, w.ap[:, kc, 0:128], h.ap[:, kc, tt * NT:(tt + 1) * NT],
                                start=(kc == 0), stop=(kc == 7)), reads=[w, hh[tt]], writes=[pg])
                        for kc in range(8):
                            P.op("tensor", lambda e, kc=kc, pu=pu, w=w, tt=tt: e.matmul(
                                pu.ap, w.ap[:, kc, 128:256], h.ap[:, kc, tt * NT:(tt + 1) * NT],
                                start=(kc == 0), stop=(kc == 7)), reads=[w, hh[tt]], writes=[pu])
                        s_ = sg[(f * 2 + tt) % 2]
                        P.op("scalar", lambda e, pg=pg, s_=s_: e.activation(s_.ap, pg.ap, AF.Silu),
                             reads=[pg], writes=[s_])
                        a_ = act[f]
                        P.op("vector", lambda e, pu=pu, s_=s_, a_=a_, tt=tt: e.tensor_tensor(
                            a_.ap[:, tt * NT:(tt + 1) * NT], s_.ap, pu.ap, ALU.mult),
                            reads=[s_, pu], writes=[a_])
                for dmc in range(8):
                    w = wo[dmc % 2]
                    P.dma("gpsimd", w.ap, wout[l, :, dmc * 128:(dmc + 1) * 128].rearrange(
                        "(f p) n -> p f n", p=128), writes=[w])
                    for tt in range(G // NT):
                        po = nps()
                        for f in range(NF):
                            P.op("tensor", lambda e, f=f, po=po, w=w, tt=tt: e.matmul(
                                po.ap, w.ap[:, f, :], act[f].ap[:, tt * NT:(tt + 1) * NT],
                                start=(f == 0), stop=(f == NF - 1)), reads=[w, act[f]], writes=[po])
                        x_ = xr[(dmc * 2 + tt) % 3]
                        P.dma("sync", x_.ap, src[dmc * 128:(dmc + 1) * 128,
                                                 t0 + tt * NT:t0 + (tt + 1) * NT],
                              reads=[src_g[g]], writes=[x_])
                        P.op("vector", lambda e, po=po, x_=x_, dmc=dmc: e.scalar_tensor_tensor(
                            x_.ap, po.ap, hg.ap[:, which * 16 + dmc:which * 16 + dmc + 1], x_.ap,
                            ALU.mult, ALU.add), reads=[po, hg, x_], writes=[x_])
                        P.dma("sync", dst[dmc * 128:(dmc + 1) * 128, t0 + tt * NT:t0 + (tt + 1) * NT],
                              x_.ap, reads=[x_], writes=[dst_g[g]])

        def phase_final(src, src_g):
            P.barrier()
            a32.reset(keep32)
            a16.reset(keep16)
            xt = [a32.alloc([8, NT], "xt%d" % i) for i in range(2)]
            tmp = a32.alloc([8, NT], "tmp")
            rstd = a32.alloc([NT], "rstd")
            sq = a16.alloc([8, NT], "sq")
            for tt in range(T // NT):
                x_ = xt[tt % 2]
                o_ = x_
                g = (tt * NT) // G
                P.dma("sync", x_.ap, src[:, tt * NT:(tt + 1) * NT].rearrange(
                    "(kc p) t -> p kc t", p=128), reads=[src_g[g]], writes=[x_])
                norm_tile(x_, NT, o_, -1, tmp, sq, rstd)
                P.op("vector", lambda e, o_=o_: e.tensor_scalar(o_.ap, o_.ap, 32.0, None, ALU.mult),
                     reads=[o_], writes=[o_])
                P.dma("sync", yT[:, tt * NT:(tt + 1) * NT].rearrange("(kc p) t -> p kc t", p=128),
                      o_.ap, reads=[o_], writes=[yT_g[g]])


        ident = a32.alloc([128], "ident")
        cstc = a32.alloc([8], "cstc")
        sel = a32.alloc([64], "sel")
        keep32 = a32.off
        P.dma("sync", ident.ap, cst[:, 0:128], writes=[ident])
        P.dma("sync", cstc.ap, cst[:, 128:136], writes=[cstc])
        P.op("vector", lambda e: e.memset(sel.ap, 0.0), writes=[sel])
        P.op("vector", lambda e: e.memset(sel.ap[64:65, :], 1.0), writes=[sel])

        def phase_tables():
            P.barrier()
            a32.reset(keep32)
            a16.reset(keep16)
            posi = Buf(st.enter_context(nc.sbuf_tensor("posi", [128, 1024], I32))[:], "posi")
            ki = Buf(st.enter_context(nc.sbuf_tensor("ki", [128, 1024], I32))[:], "ki")
            posf = a32.alloc([1024], "posf")
            ang = a32.alloc([1024], "ang")
            tq = a32.alloc([1024], "tq")
            kf = a32.alloc([1024], "kf")
            msk = a32.alloc([1024], "msk")
            TWO_PI = 2.0 * np.pi
            for blk in range(4):
                c0 = blk * 1024
                P.dma("sync", posi.ap, pos[0:1, c0:c0 + 1024].partition_broadcast(128), writes=[posi])
                P.op("vector", lambda e: e.tensor_copy(posf.ap, posi.ap), reads=[posi], writes=[posf])
                for ti in range(4):
                    invc = 0 if ti < 2 else 2
                    sgnc = 1 if ti < 2 else 3
                    off = (np.pi / 2.0) if ti % 2 == 0 else 0.0
                    P.op("vector", lambda e, invc=invc, off=off: e.tensor_scalar(
                        ang.ap, posf.ap, cstc.ap[:, invc:invc + 1], off, ALU.mult, ALU.add),
                        reads=[posf, cstc], writes=[ang])
                    P.op("vector", lambda e: e.tensor_scalar(tq.ap, ang.ap, 1.0 / TWO_PI, None, ALU.mult),
                         reads=[ang], writes=[tq])
                    P.op("vector", lambda e: e.tensor_copy(ki.ap, tq.ap), reads=[tq], writes=[ki])
                    P.op("vector", lambda e: e.tensor_copy(kf.ap, ki.ap), reads=[ki], writes=[kf])
                    P.op("vector", lambda e: e.scalar_tensor_tensor(
                        ang.ap, kf.ap, -TWO_PI, ang.ap, ALU.mult, ALU.add), reads=[kf, ang], writes=[ang])
                    P.op("vector", lambda e: e.tensor_scalar(msk.ap, ang.ap, float(np.pi), -TWO_PI,
                                                             ALU.is_gt, ALU.mult),
                         reads=[ang], writes=[msk])
                    P.op("vector", lambda e: e.tensor_tensor(ang.ap, ang.ap, msk.ap, ALU.add),
                         reads=[ang, msk], writes=[ang])
                    P.op("vector", lambda e: e.tensor_scalar(msk.ap, ang.ap, float(-np.pi), TWO_PI,
                                                             ALU.is_lt, ALU.mult),
                         reads=[ang], writes=[msk])
                    P.op("vector", lambda e: e.tensor_tensor(ang.ap, ang.ap, msk.ap, ALU.add),
                         reads=[ang, msk], writes=[ang])
                    P.op("scalar", lambda e: e.activation(tq.ap, ang.ap, AF.Sin), reads=[ang], writes=[tq])
                    if ti % 2 == 1:
                        P.op("vector", lambda e, sgnc=sgnc: e.tensor_scalar(
                            tq.ap, tq.ap, cstc.ap[:, sgnc:sgnc + 1], None, ALU.mult),
                            reads=[tq, cstc], writes=[tq])
                    P.dma("sync", tabs[ti, :, c0:c0 + 1024], tq.ap, reads=[tq], writes=[tabs_b])

        ZB = 1152
        RB = 1152 + 672
        zchunks = []
        for c in range(9):
            zchunks.append(dict(M=128, pieces=[(c * 128, 128, 0)], shift=True))
        for c in range(3):
            zchunks.append(dict(M=128, pieces=[(ZB + c * 128, 128, 0)]))
        for c in range(2):
            zchunks.append(dict(M=128, pieces=[(ZB + 384 + c * 128, 128, 0)]))
        zchunks.append(dict(M=96, pieces=[(ZB + 640, 32, 64)], rot=("M", [(ZB + 656, 16, 64), (ZB + 640, 16, 80)], 1.0)))
        zchunks.append(None)
        for part, sc in ((0, 1.0), (1, 0.125)):
            for c in range(2):
                b0 = RB + part * 256 + c * 128
                zchunks.append(dict(M=128, pieces=[(b0, 128, 0)],
                                    rot=("R", [(b0 + 32, 32, 0), (b0, 32, 32), (b0 + 96, 32, 64), (b0 + 64, 32, 96)], sc)))
        for part in (2, 3):
            for c in range(2):
                zchunks.append(dict(M=128, pieces=[(RB + part * 256 + c * 128, 128, 0)]))

        def phase_z(l):
            P.barrier()
            a32.reset(keep32)
            a16.reset(keep16)
            xt = a32.alloc([8, NT], "xt")
            tmp = a32.alloc([8, NT], "tmp")
            rstd = a32.alloc([NT], "rstd")
            xh = a32.alloc([8, 2], "xh")
            tmph = a32.alloc([8, 2], "tmph")
            rstdh = a32.alloc([2], "rstdh")
            mub = a32.alloc([1152], "mub")
            om = a32.alloc([1152], "om")
            hm = a32.alloc([1152], "hm")
            ev = [a32.alloc([NT], "ev%d" % i) for i in range(3)]
            tb = [a32.alloc([NT], "tb%d" % i) for i in range(4)]
            sq = a16.alloc([8, NT], "sq")
            sqh = a16.alloc([8, 2], "sqh")
            h = a16.alloc([8, NT], "h")
            hh = a16.alloc([8, 2], "hh")
            hs = a16.alloc([8, NT], "hs")
            W = [a16.alloc([8, 128], "W%d" % i) for i in range(4)]
            W1 = [a16.alloc([8, 128], "W1%d" % i) for i in range(2)]
            W2 = [a16.alloc([8, 128], "W2%d" % i) for i in range(2)]
            wi = [0]
            P.op("vector", lambda e: e.memset(epsc.ap[:, 1:2], 1024.0 * EPS), writes=[epsc])
            P.dma("sync", mub.ap, rwkv_mu[l:l + 1, :].partition_broadcast(128), writes=[mub])
            P.op("vector", lambda e: e.tensor_scalar(om.ap, mub.ap, -1.0, 1.0, ALU.mult, ALU.add),
                 reads=[mub], writes=[om])
            P.op("vector", lambda e: e.tensor_scalar(hm.ap, mub.ap, 0.5, None, ALU.mult),
                 reads=[mub], writes=[hm])
            for i in range(4):
                P.op("vector", lambda e, i=i: e.memset(W[i].ap, 0.0), writes=[W[i]])

            def loadW(pieces):
                w = W[wi[0] % 4]
                wi[0] += 1
                for (sc_, n_, dc_) in pieces:
                    P.dma("gpsimd", w.ap[:, :, dc_:dc_ + n_], w_in[l, :, sc_:sc_ + n_].rearrange(
                        "(kc p) n -> p kc n", p=128), writes=[w])
                return w
            for tt in range(T // NT):
                t0 = tt * NT
                g = t0 // G
                P.dma("sync", xt.ap, xs[:, t0:t0 + NT].rearrange("(kc p) t -> p kc t", p=128),
                      reads=[xs_g[g]], writes=[xt])
                P.op("vector", lambda e: e.memset(xh.ap, 1.0), writes=[xh])
                if tt > 0:
                    P.dma("sync", xh.ap[:, :, 0:1], xs[:, t0 - 1:t0].rearrange("(kc p) t -> p kc t", p=128),
                          reads=[xs_g[(t0 - 1) // G]], writes=[xh], allow_slow_non_contiguous=True)
                if tt < T // NT - 1:
                    P.dma("sync", xh.ap[:, :, 1:2], xs[:, t0 + NT:t0 + NT + 1].rearrange(
                        "(kc p) t -> p kc t", p=128), reads=[xs_g[(t0 + NT) // G]], writes=[xh],
                          allow_slow_non_contiguous=True)
                norm_tile(xt, NT, h, 1, tmp, sq, rstd)
                norm_tile(xh, 2, hh, 1, tmph, sqh, rstdh)
                if tt == 0:
                    P.op("vector", lambda e: e.memset(hh.ap[:, :, 0:1], 0.0), writes=[hh])
                if tt == T // NT - 1:
                    P.op("vector", lambda e: e.memset(hh.ap[:, :, 1:2], 0.0), writes=[hh])
                P.op("vector", lambda e: e.tensor_tensor(hs.ap[:, :, 1:NT - 1], h.ap[:, :, 0:NT - 2],
                                                         h.ap[:, :, 2:NT], ALU.add), reads=[h], writes=[hs])
                P.op("vector", lambda e: e.tensor_tensor(hs.ap[:, :, 0:1], hh.ap[:, :, 0:1],
                                                         h.ap[:, :, 1:2], ALU.add), reads=[h, hh], writes=[hs])
                P.op("vector", lambda e: e.tensor_tensor(hs.ap[:, :, NT - 1:NT], h.ap[:, :, NT - 2:NT - 1],
                                                         hh.ap[:, :, 1:2], ALU.add), reads=[h, hh], writes=[hs])
                for ti in range(4):
                    P.dma("sync", tb[ti].ap, tabs[ti, :, t0:t0 + NT], reads=[tabs_b], writes=[tb[ti]])
                for ci, ch in enumerate(zchunks):
                    if ch is None:
                        continue
                    M = ch["M"]
                    w = loadW(ch["pieces"])
                    ps = nps()
                    e_ = ev[ci % 3]
                    if ch.get("shift"):
                        c0 = ch["pieces"][0][0]
                        w1 = W1[ci % 2]
                        w2 = W2[ci % 2]
                        for kc in range(8):
                            P.op("vector", lambda e, kc=kc, w=w, w1=w1, c0=c0: e.tensor_tensor(
                                w1.ap[:, kc, :], w.ap[:, kc, :], om.ap[:, c0:c0 + 128], ALU.mult),
                                reads=[w, om], writes=[w1])
                            P.op("vector", lambda e, kc=kc, w=w, w2=w2, c0=c0: e.tensor_tensor(
                                w2.ap[:, kc, :], w.ap[:, kc, :], hm.ap[:, c0:c0 + 128], ALU.mult),
                                reads=[w, hm], writes=[w2])
                        for kc in range(8):
                            P.op("tensor", lambda e, kc=kc, ps=ps, w1=w1: e.matmul(
                                ps.ap, w1.ap[:, kc, :], h.ap[:, kc, :], start=(kc == 0), stop=False),
                                reads=[w1, h], writes=[ps])
                        for kc in range(8):
                            P.op("tensor", lambda e, kc=kc, ps=ps, w2=w2: e.matmul(
                                ps.ap, w2.ap[:, kc, :], hs.ap[:, kc, :], start=False, stop=(kc == 7)),
                                reads=[w2, hs], writes=[ps])
                        P.op("scalar", lambda e, ps=ps, e_=e_: e.activation(e_.ap, ps.ap, AF.Identity),
                             reads=[ps], writes=[e_])
                    else:
                        for kc in range(8):
                            P.op("tensor", lambda e, kc=kc, ps=ps, w=w, M=M: e.matmul(
                                ps.ap[0:M, :], w.ap[:, kc, 0:M], h.ap[:, kc, :], start=(kc == 0), stop=(kc == 7)),
                                reads=[w, h], writes=[ps])
                        rot = ch.get("rot")
                        if rot is None:
                            P.op("scalar", lambda e, ps=ps, e_=e_, M=M: e.activation(
                                e_.ap[0:M, :], ps.ap[0:M, :], AF.Identity), reads=[ps], writes=[e_])
                        else:
                            kind, pp, sc_ = rot
                            if kind == "M":
                                wB = W[wi[0] % 4]
                                wi[0] += 1
                                for (s2, n2, d2) in pp:
                                    P.dma("gpsimd", wB.ap[:, :, d2:d2 + n2], w_in[l, :, s2:s2 + n2].rearrange(
                                        "(kc p) n -> p kc n", p=128), writes=[wB])
                                lo, hi = 64, 96
                                tcs, tsn = tb[2], tb[3]
                            else:
                                wB = loadW(pp)
                                lo, hi = 0, 128
                                tcs, tsn = tb[0], tb[1]
                            ps2 = nps()
                            for kc in range(8):
                                P.op("tensor", lambda e, kc=kc, ps2=ps2, wB=wB, M=M: e.matmul(
                                    ps2.ap[0:M, :], wB.ap[:, kc, 0:M], h.ap[:, kc, :], start=(kc == 0), stop=(kc == 7)),
                                    reads=[wB, h], writes=[ps2])
                            e2 = ev[(ci + 1) % 3]
                            P.op("vector", lambda e, ps=ps, e_=e_, tcs=tcs, lo=lo, hi=hi, sc_=sc_: e.scalar_tensor_tensor(
                                e_.ap[lo:hi, :], ps.ap[lo:hi, :], float(sc_), tcs.ap[lo:hi, :], ALU.mult, ALU.mult),
                                reads=[ps, tcs], writes=[e_])
                            P.op("vector", lambda e, ps2=ps2, e2=e2, tsn=tsn, lo=lo, hi=hi, sc_=sc_: e.scalar_tensor_tensor(
                                e2.ap[lo:hi, :], ps2.ap[lo:hi, :], float(sc_), tsn.ap[lo:hi, :], ALU.mult, ALU.mult),
                                reads=[ps2, tsn], writes=[e2])
                            P.op("vector", lambda e, e_=e_, e2=e2, lo=lo, hi=hi: e.tensor_tensor(
                                e_.ap[lo:hi, :], e_.ap[lo:hi, :], e2.ap[lo:hi, :], ALU.add),
                                reads=[e_, e2], writes=[e_])
                            if kind == "M":
                                P.dma("sync", zT[ci * 128 + 64:ci * 128 + 96, t0:t0 + NT], e_.ap[64:96, :],
                                      reads=[e_], writes=[zT_b])
                                continue
                    P.dma("sync", zT[ci * 128:ci * 128 + M, t0:t0 + NT], e_.ap[0:M, :],
                          reads=[e_], writes=[zT_b])

        def phase_mla(l):
            P.barrier()
            a32.reset(keep32)
            a16.reset(keep16)
            SCL = 96.0 ** -0.5
            wuq = a16.alloc([3, 768], "wuq")
            wsw = a16.alloc([3, 768], "wsw")
            wkn = a16.alloc([2, 512], "wkn")
            wv = a16.alloc([2, 512], "wv")
            vaug = a16.alloc([32, 520], "vaug")
            cqb = a16.alloc([3, NT], "cqb")
            ckvb = a16.alloc([2, NT], "ckvb")
            sqq = a16.alloc([3, NT], "sqq")
            sqk = a16.alloc([2, NT], "sqk")
            qc = [a16.alloc([NT], "qc%d" % i) for i in range(2)]
            kcb = [a16.alloc([NT], "kcb%d" % i) for i in range(2)]
            qg = a32.alloc([3], "qg")
            kvg = a32.alloc([2], "kvg")
            cq = a32.alloc([3, NT], "cq")
            ckv = a32.alloc([2, NT], "ckv")
            rq = a32.alloc([NT], "rq")
            rk = a32.alloc([NT], "rk")
            rc = a32.alloc([NT], "rc")
            rs = a32.alloc([NT], "rs")
            kpe = a32.alloc([NT], "kpe")
            t1 = a32.alloc([NT], "t1")
            tcm = a32.alloc([NT], "tcm")
            tsm = a32.alloc([NT], "tsm")
            rtok = a32.alloc([4], "rtok")
            osb = [a32.alloc([NT], "osb%d" % i) for i in range(2)]
            rec = a32.alloc([NT], "rec")
            P.dma("sync", qg.ap, mla_qgT[l], writes=[qg])
            P.dma("sync", kvg.ap, mla_kvgT[l], writes=[kvg])
            P.dma("gpsimd", wuq.ap, mla_w_uq[l].rearrange("(kc p) n -> p kc n", p=128), writes=[wuq])
            for kc in range(2):
                srcw = mla_w_ukv[l, kc * 128:(kc + 1) * 128, :].rearrange("p (h c) -> p h c", c=128)
                P.dma("gpsimd", wkn.ap[:, kc, :].rearrange("p (h c) -> p h c", c=64), srcw[:, :, 0:64], writes=[wkn])
                P.dma("gpsimd", wv.ap[:, kc, :].rearrange("p (h c) -> p h c", c=64), srcw[:, :, 64:128], writes=[wv])
            for kc in range(3):
                P.op("vector", lambda e, kc=kc: e.tensor_scalar(wuq.ap[:, kc, :], wuq.ap[:, kc, :],
                                                                qg.ap[:, kc:kc + 1], None, ALU.mult),
                     reads=[wuq, qg], writes=[wuq])
            for kc in range(2):
                P.op("vector", lambda e, kc=kc: e.tensor_scalar(wkn.ap[:, kc, :], wkn.ap[:, kc, :],
                                                                kvg.ap[:, kc:kc + 1], None, ALU.mult),
                     reads=[wkn, kvg], writes=[wkn])
                P.op("vector", lambda e, kc=kc: e.tensor_scalar(wv.ap[:, kc, :], wv.ap[:, kc, :],
                                                                kvg.ap[:, kc:kc + 1], None, ALU.mult),
                     reads=[wv, kvg], writes=[wv])
            P.op("vector", lambda e: e.memset(wsw.ap, 0.0), writes=[wsw])
            w4 = wuq.ap.rearrange("p k (h c) -> p k h c", c=96)
            s4 = wsw.ap.rearrange("p k (h c) -> p k h c", c=96)
            for kc in range(3):
                P.op("vector", lambda e, kc=kc: e.tensor_copy(s4[:, kc, :, 64:80], w4[:, kc, :, 80:96]),
                     reads=[wuq], writes=[wsw])
                P.op("vector", lambda e, kc=kc: e.tensor_copy(s4[:, kc, :, 80:96], w4[:, kc, :, 64:80]),
                     reads=[wuq], writes=[wsw])
            P.op("vector", lambda e: e.memset(vaug.ap, 1.0), writes=[vaug])
            P.op("vector", lambda e: e.memset(epsc.ap[:, 2:3], EPS), writes=[epsc])
            for tt in range(T // NT):
                t0 = tt * NT
                P.dma("sync", cq.ap, zT[9 * 128:12 * 128, t0:t0 + NT].rearrange("(kc p) t -> p kc t", p=128),
                      reads=[zT_b], writes=[cq])
                P.dma("sync", ckv.ap, zT[12 * 128:14 * 128, t0:t0 + NT].rearrange("(kc p) t -> p kc t", p=128),
                      reads=[zT_b], writes=[ckv])
                P.dma("sync", kpe.ap[64:96, :], zT[14 * 128 + 64:14 * 128 + 96, t0:t0 + NT],
                      reads=[zT_b], writes=[kpe])
                P.dma("sync", tcm.ap[64:96, :], tabs[2, 64:96, t0:t0 + NT], reads=[tabs_b], writes=[tcm])
                P.dma("sync", tsm.ap[64:96, :], tabs[3, 64:96, t0:t0 + NT], reads=[tabs_b], writes=[tsm])
                P.op("scalar", lambda e: e.activation(sqq.ap, cq.ap, AF.Square), reads=[cq], writes=[sqq])
                P.op("scalar", lambda e: e.activation(sqk.ap, ckv.ap, AF.Square), reads=[ckv], writes=[sqk])
                P.op("scalar", lambda e: e.activation(cqb.ap, cq.ap, AF.Identity), reads=[cq], writes=[cqb])
                P.op("scalar", lambda e: e.activation(ckvb.ap, ckv.ap, AF.Identity), reads=[ckv], writes=[ckvb])
                ps = nps()
                for kc in range(3):
                    P.op("tensor", lambda e, kc=kc, ps=ps: e.matmul(ps.ap, ones16.ap, sqq.ap[:, kc, :],
                                                                    start=(kc == 0), stop=(kc == 2)),
                         reads=[sqq, ones16], writes=[ps])
                P.op("scalar", lambda e, ps=ps: e.activation(rq.ap, ps.ap, AF.Sqrt, bias=epsc.ap[:, 2:3],
                                                             scale=1.0 / 384.0), reads=[ps, epsc], writes=[rq])
                P.op("vector", lambda e: e.reciprocal(rq.ap, rq.ap), reads=[rq], writes=[rq])
                P.op("vector", lambda e: e.tensor_scalar(rq.ap, rq.ap, SCL, None, ALU.mult), reads=[rq], writes=[rq])
                ps = nps()
                for kc in range(2):
                    P.op("tensor", lambda e, kc=kc, ps=ps: e.matmul(ps.ap, ones16.ap, sqk.ap[:, kc, :],
                                                                    start=(kc == 0), stop=(kc == 1)),
                         reads=[sqk, ones16], writes=[ps])
                P.op("scalar", lambda e, ps=ps: e.activation(rk.ap, ps.ap, AF.Sqrt, bias=epsc.ap[:, 2:3],
                                                             scale=1.0 / 256.0), reads=[ps, epsc], writes=[rk])
                P.op("vector", lambda e: e.reciprocal(rk.ap, rk.ap), reads=[rk], writes=[rk])
                P.op("vector", lambda e: e.tensor_tensor(rc.ap[64:96, :], rq.ap[64:96, :], tcm.ap[64:96, :], ALU.mult),
                     reads=[rq, tcm], writes=[rc])
                P.op("vector", lambda e: e.tensor_tensor(rs.ap[64:96, :], rq.ap[64:96, :], tsm.ap[64:96, :], ALU.mult),
                     reads=[rq, tsm], writes=[rs])
                for hd in range(8):
                    psq = nps()
                    pss = nps()
                    for kc in range(3):
                        P.op("tensor", lambda e, kc=kc, psq=psq, hd=hd: e.matmul(
                            psq.ap[0:96, :], wuq.ap[:, kc, hd * 96:(hd + 1) * 96], cqb.ap[:, kc, :],
                            start=(kc == 0), stop=(kc == 2)), reads=[wuq, cqb], writes=[psq])
                    for kc in range(3):
                        P.op("tensor", lambda e, kc=kc, pss=pss, hd=hd: e.matmul(
                            pss.ap[0:96, :], wsw.ap[:, kc, hd * 96:(hd + 1) * 96], cqb.ap[:, kc, :],
                            start=(kc == 0), stop=(kc == 2)), reads=[wsw, cqb], writes=[pss])
                    q_ = qc[hd % 2]
                    P.op("vector", lambda e, psq=psq, q_=q_: e.tensor_tensor(
                        q_.ap[0:64, :], psq.ap[0:64, :], rq.ap[0:64, :], ALU.mult), reads=[psq, rq], writes=[q_])
                    P.op("vector", lambda e, psq=psq: e.tensor_tensor(
                        t1.ap[64:96, :], psq.ap[64:96, :], rc.ap[64:96, :], ALU.mult), reads=[psq, rc], writes=[t1])
                    P.op("vector", lambda e, pss=pss: e.tensor_tensor(
                        rec.ap[64:96, :], pss.ap[64:96, :], rs.ap[64:96, :], ALU.mult), reads=[pss, rs], writes=[rec])
                    P.op("vector", lambda e, q_=q_: e.tensor_tensor(
                        q_.ap[64:96, :], t1.ap[64:96, :], rec.ap[64:96, :], ALU.add), reads=[t1, rec], writes=[q_])
                    P.dma("sync", qT[hd, :, t0:t0 + NT], q_.ap[0:96, :], reads=[q_], writes=[qT_b])
                    psk = nps()
                    for kc in range(2):
                        P.op("tensor", lambda e, kc=kc, psk=psk, hd=hd: e.matmul(
                            psk.ap[0:64, :], wkn.ap[:, kc, hd * 64:(hd + 1) * 64], ckvb.ap[:, kc, :],
                            start=(kc == 0), stop=(kc == 1)), reads=[wkn, ckvb], writes=[psk])
                    k_ = kcb[hd % 2]
                    P.op("vector", lambda e, psk=psk, k_=k_: e.tensor_tensor(
                        k_.ap[0:64, :], psk.ap[0:64, :], rk.ap[0:64, :], ALU.mult), reads=[psk, rk], writes=[k_])
                    P.op("scalar", lambda e, k_=k_: e.activation(k_.ap[64:96, :], kpe.ap[64:96, :], AF.Identity),
                         reads=[kpe], writes=[k_])
                    P.dma("sync", kT[hd, :, t0:t0 + NT], k_.ap[0:96, :], reads=[k_], writes=[kT_b])
                for s_ in range(4):
                    kt = tt * 4 + s_
                    pst = nps()
                    for kc in range(2):
                        P.op("tensor", lambda e, kc=kc, pst=pst, s_=s_: e.matmul(
                            pst.ap[:, 0:1], sqk.ap[:, kc, s_ * 128:(s_ + 1) * 128], ones16.ap[:, 0:1],
                            start=(kc == 0), stop=(kc == 1)), reads=[sqk, ones16], writes=[pst])
                    P.op("scalar", lambda e, pst=pst, s_=s_: e.activation(
                        rtok.ap[:, s_:s_ + 1], pst.ap[:, 0:1], AF.Sqrt, bias=epsc.ap[:, 2:3], scale=1.0 / 256.0),
                        reads=[pst, epsc], writes=[rtok])
                    P.op("vector", lambda e, s_=s_: e.reciprocal(rtok.ap[:, s_:s_ + 1], rtok.ap[:, s_:s_ + 1]),
                         reads=[rtok], writes=[rtok])
                    psv = nps()
                    for kc in range(2):
                        P.op("tensor", lambda e, kc=kc, psv=psv, s_=s_: e.matmul(
                            psv.ap, ckvb.ap[:, kc, s_ * 128:(s_ + 1) * 128], wv.ap[:, kc, :],
                            start=(kc == 0), stop=(kc == 1)), reads=[ckvb, wv], writes=[psv])
                    P.op("vector", lambda e, psv=psv, s_=s_, kt=kt: e.tensor_scalar(
                        vaug.ap[:, kt, :].rearrange("p (h c) -> p h c", c=65)[:, :, 0:64],
                        psv.ap.rearrange("p (h c) -> p h c", c=64), rtok.ap[:, s_:s_ + 1], None, ALU.mult),
                        reads=[psv, rtok], writes=[vaug])
            Kt = a16.alloc([T], "Kt")
            Qt = a16.alloc([T], "Qt")
            pT = [a16.alloc([NT], "pT%d" % i) for i in range(3)]
            ob = [a16.alloc([NT], "ob%d" % i) for i in range(2)]
            pso = psb[6]
            psd = psb[7]
            for hd in range(8):
                P.dma("sync", Kt.ap[0:96, :], kT[hd], reads=[kT_b], writes=[Kt])
                P.dma("sync", Qt.ap[0:96, :], qT[hd], reads=[qT_b], writes=[Qt])
                for j in range(T // NT):
                    for i in range(32):
                        ps = nps()
                        P.op("tensor", lambda e, ps=ps, i=i, j=j: e.matmul(
                            ps.ap, Kt.ap[0:96, i * 128:(i + 1) * 128], Qt.ap[0:96, j * NT:(j + 1) * NT],
                            start=True, stop=True), reads=[Kt, Qt], writes=[ps])
                        p_ = pT[i % 3]
                        P.op("scalar", lambda e, ps=ps, p_=p_: e.activation(p_.ap, ps.ap, AF.Exp),
                             reads=[ps], writes=[p_])
                        P.op("tensor", lambda e, p_=p_, i=i, hd=hd: e.matmul(
                            pso.ap[0:65, :], vaug.ap[:, i, hd * 65:(hd + 1) * 65], p_.ap,
                            start=(i == 0), stop=(i == 31)), reads=[vaug, p_], writes=[pso])
                    o_ = osb[j % 2]
                    P.op("scalar", lambda e, o_=o_: e.activation(o_.ap[0:65, :], pso.ap[0:65, :], AF.Identity),
                         reads=[pso], writes=[o_])
                    P.op("tensor", lambda e, o_=o_: e.matmul(psd.ap[0:64, :], sel.ap[0:65, :], o_.ap[0:65, :],
                                                            start=True, stop=True), reads=[sel, o_], writes=[psd])
                    P.op("vector", lambda e: e.reciprocal(rec.ap[0:64, :], psd.ap[0:64, :]), reads=[psd], writes=[rec])
                    b_ = ob[j % 2]
                    P.op("vector", lambda e, o_=o_, b_=b_: e.tensor_tensor(
                        b_.ap[0:64, :], o_.ap[0:64, :], rec.ap[0:64, :], ALU.mult), reads=[o_, rec], writes=[b_])
                    P.dma("sync", oT[256 + hd * 64:256 + (hd + 1) * 64, j * NT:(j + 1) * NT], b_.ap[0:64, :],
                          reads=[b_], writes=[oT_b])


        def phase_ret(l):
            P.barrier()
            a32.reset(keep32)
            a16.reset(keep16)
            idb = a16.alloc([128], "idb")
            vtok = a16.alloc([32, 256], "vtok")
            Kc = a16.alloc([T], "Kc")
            Qc = a16.alloc([T], "Qc")
            vT = a16.alloc([T], "vT")
            pT = [a16.alloc([NT], "pT%d" % i) for i in range(3)]
            ob = [a16.alloc([NT], "ob%d" % i) for i in range(2)]
            base = a32.alloc([NT], "base")
            dl = a32.alloc([64], "dl")
            lg = a32.alloc([8], "lg")
            nlg = a32.alloc([8], "nlg")
            bf_ = a32.alloc([64], "bf")
            bb_ = a32.alloc([64], "bb")
            a64 = a32.alloc([64], "a64")
            gng = a32.alloc([4], "gng")
            Dst = [a32.alloc([NT], "Dst%d" % i) for i in range(4)]
            Dt = [a32.alloc([NT], "Dt%d" % i) for i in range(3)]
            msk = a32.alloc([NT], "msk")
            e2 = a32.alloc([NT], "e2")
            ysb = a32.alloc([NT], "ysb")
            dd = a32.alloc([NT], "dd")
            sqd = a32.alloc([NT], "sqd")
            rsd = a32.alloc([NT], "rsd")
            gt = [a32.alloc([NT], "gt%d" % i) for i in range(2)]
            pso = psb[6]
            psd = psb[7]
            P.dma("sync", base.ap, basec[:, 0:512], writes=[base])
            P.dma("sync", dl.ap, basec[:, 512:576], writes=[dl])
            P.dma("sync", lg.ap, ret_lr[l].partition_broadcast(128), writes=[lg])
            P.dma("sync", gng.ap[0:64, :], ret_gT[l], writes=[gng])
            P.op("scalar", lambda e: e.activation(lg.ap, lg.ap, AF.Exp), reads=[lg], writes=[lg])
            P.op("vector", lambda e: e.tensor_scalar(nlg.ap, lg.ap, 1.0, None, ALU.mult), reads=[lg], writes=[nlg])
            P.op("vector", lambda e: e.tensor_scalar(lg.ap, lg.ap, -1.0, None, ALU.mult), reads=[lg], writes=[lg])
            P.op("vector", lambda e: e.memset(a64.ap, 1.0 / 64.0), writes=[a64])
            P.op("vector", lambda e: e.memset(epsc.ap[:, 3:4], 1e-5), writes=[epsc])
            P.op("vector", lambda e: e.tensor_copy(idb.ap, ident.ap), reads=[ident], writes=[idb])
            for c in range(2):
                P.dma("gpsimd", vT.ap, zT[(20 + c) * 128:(21 + c) * 128, :], reads=[zT_b], writes=[vT])
                for kt in range(32):
                    ps = nps()
                    P.op("tensor", lambda e, ps=ps, kt=kt: e.matmul(
                        ps.ap[:, 0:128], vT.ap[:, kt * 128:(kt + 1) * 128], idb.ap, start=True, stop=True),
                        reads=[vT, idb], writes=[ps])
                    P.op("scalar", lambda e, ps=ps, kt=kt, c=c: e.activation(
                        vtok.ap[:, kt, c * 128:(c + 1) * 128], ps.ap[:, 0:128], AF.Identity),
                        reads=[ps], writes=[vtok])
            for hd in range(4):
                c = hd // 2
                ro = (hd % 2) * 64
                if hd % 2 == 0:
                    P.dma("gpsimd", Qc.ap, zT[(16 + c) * 128:(17 + c) * 128, :], reads=[zT_b], writes=[Qc])
                    P.dma("gpsimd", Kc.ap, zT[(18 + c) * 128:(19 + c) * 128, :], reads=[zT_b], writes=[Kc])
                lf = lg.ap[:, hd:hd + 1]
                nlb = nlg.ap[:, 4 + hd:5 + hd]
                P.op("vector", lambda e, lf=lf: e.tensor_scalar(bf_.ap, dl.ap, lf, None, ALU.mult),
                     reads=[dl, lg], writes=[bf_])
                P.op("vector", lambda e, nlb=nlb: e.tensor_scalar(bb_.ap, dl.ap, nlb, None, ALU.mult),
                     reads=[dl, nlg], writes=[bb_])
                for si in range(4):
                    dlt = -128.0 * si
                    di = int((dlt + 3968) // 128)
                    D_ = Dst[si]
                    P.op("scalar", lambda e, D_=D_, di=di, lf=lf: e.activation(
                        D_.ap, base.ap, AF.Exp, bias=bf_.ap[:, di:di + 1], scale=lf), reads=[base, bf_, lg], writes=[D_])
                    P.op("vector", lambda e, dlt=dlt: e.tensor_scalar(msk.ap, base.ap, dlt, 0.0, ALU.add, ALU.is_ge),
                         reads=[base], writes=[msk])
                    P.op("vector", lambda e, D_=D_: e.tensor_tensor(D_.ap, D_.ap, msk.ap, ALU.mult),
                         reads=[D_, msk], writes=[D_])
                    P.op("scalar", lambda e, di=di, nlb=nlb: e.activation(
                        e2.ap, base.ap, AF.Exp, bias=bb_.ap[:, di:di + 1], scale=nlb), reads=[base, bb_, nlg], writes=[e2])
                    P.op("vector", lambda e, dlt=dlt: e.tensor_scalar(msk.ap, base.ap, dlt, 0.0, ALU.add, ALU.is_lt),
                         reads=[base], writes=[msk])
                    P.op("vector", lambda e: e.tensor_tensor(e2.ap, e2.ap, msk.ap, ALU.mult),
                         reads=[e2, msk], writes=[e2])
                    P.op("vector", lambda e, D_=D_: e.tensor_tensor(D_.ap, D_.ap, e2.ap, ALU.add),
                         reads=[D_, e2], writes=[D_])
                for j in range(T // NT):
                    g_ = gt[j % 2]
                    P.dma("sync", g_.ap[0:64, :], zT[(22 + c) * 128 + ro:(22 + c) * 128 + ro + 64, j * NT:(j + 1) * NT],
                          reads=[zT_b], writes=[g_])
                    P.op("scalar", lambda e, g_=g_: e.activation(g_.ap[0:64, :], g_.ap[0:64, :], AF.Silu),
                         reads=[g_], writes=[g_])
                    for i in range(32):
                        dlt = 512 * j - 128 * i
                        di = int((dlt + 3968) // 128)
                        ps = nps()
                        P.op("tensor", lambda e, ps=ps, i=i, j=j, ro=ro: e.matmul(
                            ps.ap, Kc.ap[ro:ro + 64, i * 128:(i + 1) * 128], Qc.ap[ro:ro + 64, j * NT:(j + 1) * NT],
                            start=True, stop=True), reads=[Kc, Qc], writes=[ps])
                        if dlt >= 127:
                            D_ = Dt[i % 3]
                            P.op("scalar", lambda e, D_=D_, di=di, lf=lf: e.activation(
                                D_.ap, base.ap, AF.Exp, bias=bf_.ap[:, di:di + 1], scale=lf),
                                reads=[base, bf_, lg], writes=[D_])
                        elif dlt <= -512:
                            D_ = Dt[i % 3]
                            P.op("scalar", lambda e, D_=D_, di=di, nlb=nlb: e.activation(
                                D_.ap, base.ap, AF.Exp, bias=bb_.ap[:, di:di + 1], scale=nlb),
                                reads=[base, bb_, nlg], writes=[D_])
                        else:
                            D_ = Dst[int(-dlt // 128)]
                        p_ = pT[i % 3]
                        P.op("vector", lambda e, ps=ps, p_=p_, D_=D_: e.tensor_tensor(p_.ap, ps.ap, D_.ap, ALU.mult),
                             reads=[ps, D_], writes=[p_])
                        P.op("tensor", lambda e, p_=p_, i=i, hd=hd: e.matmul(
                            pso.ap[0:64, :], vtok.ap[:, i, hd * 64:(hd + 1) * 64], p_.ap,
                            start=(i == 0), stop=(i == 31)), reads=[vtok, p_], writes=[pso])
                    P.op("scalar", lambda e: e.activation(ysb.ap[0:64, :], pso.ap[0:64, :], AF.Identity),
                         reads=[pso], writes=[ysb])
                    P.op("tensor", lambda e: e.matmul(psd.ap[0:64, :], a64.ap[0:64, :], ysb.ap[0:64, :],
                                                      start=True, stop=True), reads=[a64, ysb], writes=[psd])
                    P.op("vector", lambda e: e.tensor_tensor(dd.ap[0:64, :], ysb.ap[0:64, :], psd.ap[0:64, :],
                                                             ALU.subtract), reads=[ysb, psd], writes=[dd])
                    P.op("scalar", lambda e: e.activation(sqd.ap[0:64, :], dd.ap[0:64, :], AF.Square),
                         reads=[dd], writes=[sqd])
                    P.op("tensor", lambda e: e.matmul(psd.ap[0:64, :], a64.ap[0:64, :], sqd.ap[0:64, :],
                                                      start=True, stop=True), reads=[a64, sqd], writes=[psd])
                    P.op("scalar", lambda e: e.activation(rsd.ap[0:64, :], psd.ap[0:64, :], AF.Sqrt,
                                                          bias=epsc.ap[0:64, 3:4], scale=1.0),
                         reads=[psd, epsc], writes=[rsd])
                    P.op("vector", lambda e: e.reciprocal(rsd.ap[0:64, :], rsd.ap[0:64, :]), reads=[rsd], writes=[rsd])
                    P.op("vector", lambda e: e.tensor_tensor(dd.ap[0:64, :], dd.ap[0:64, :], rsd.ap[0:64, :], ALU.mult),
                         reads=[dd, rsd], writes=[dd])
                    b_ = ob[j % 2]
                    P.op("vector", lambda e, b_=b_, g_=g_, hd=hd: e.scalar_tensor_tensor(
                        b_.ap[0:64, :], dd.ap[0:64, :], gng.ap[0:64, hd:hd + 1], g_.ap[0:64, :], ALU.mult, ALU.mult),
                        reads=[dd, gng, g_], writes=[b_])
                    P.dma("sync", oT[768 + hd * 64:768 + (hd + 1) * 64, j * NT:(j + 1) * NT], b_.ap[0:64, :],
                          reads=[b_], writes=[oT_b])


        def phase_rwkv(l):
            P.barrier()
            a32.reset(keep32)
            a16.reset(keep16)
            C = 64
            NCH = T // C
            rp = a32.alloc([18], "rp")
            omka = a32.alloc([2], "omka")
            Upre = a32.alloc([128], "Upre")
            Usuf = a32.alloc([128], "Usuf")
            blk = a32.alloc([128], "blk")
            blk64 = a32.alloc([128], "blk64")
            c12 = a32.alloc([2], "c12")
            wup = a16.alloc([256], "wup")
            aup = a16.alloc([256], "aup")
            gup = a16.alloc([256], "gup")
            sgb = a16.alloc([NT], "sgb")
            twb = a16.alloc([NT], "twb")
            alb = a16.alloc([NT], "alb")
            k16 = a16.off
            P.dma("sync", rp.ap, rwp[l], writes=[rp])
            P.dma("sync", Upre.ap, ucst[:, 0:128], writes=[Upre])
            P.dma("sync", Usuf.ap, ucst[:, 128:256], writes=[Usuf])
            P.dma("sync", blk.ap, ucst[:, 256:384], writes=[blk])
            P.dma("gpsimd", wup.ap, rw_wup[l], writes=[wup])
            P.dma("gpsimd", aup.ap, rw_aup[l], writes=[aup])
            P.dma("gpsimd", gup.ap, rw_gup[l], writes=[gup])
            P.op("vector", lambda e: e.tensor_scalar(blk64.ap, blk.ap, 1.0 / 64.0, None, ALU.mult), reads=[blk], writes=[blk64])
            P.op("vector", lambda e: e.tensor_scalar(omka.ap, rp.ap[:, 10:12], -1.0, 1.0, ALU.mult, ALU.add),
                 reads=[rp], writes=[omka])
            P.op("vector", lambda e: e.memset(c12.ap[:, 0:1], 64e-5), writes=[c12])
            kp32 = a32.off
            names = ["rT", "kT_", "vT_", "gl", "wl", "al", "kraw", "sqk", "kk", "a_d", "lw", "lwt", "LP",
                     "eP", "ePm", "eN", "bb", "km", "tmp1", "ev"]
            tl = {n: a32.alloc([NT], n) for n in names}
            pk = a32.alloc([8, 392], "pk")
            rT, kT_, vT_, gl, wl, al = (tl[n] for n in names[0:6])
            kraw, sqk, kk, a_d, lw, lwt, LP = (tl[n] for n in names[6:13])
            eP, ePm, eN, bb, km, tmp1, ev = (tl[n] for n in names[13:20])

            def v3(b):
                return b.ap.rearrange("p (c t) -> p c t", t=C)
            for tt in range(T // NT):
                t0 = tt * NT
                P.dma("sync", gl.ap, zT[6 * 128:7 * 128, t0:t0 + NT], reads=[zT_b], writes=[gl])
                P.dma("sync", wl.ap, zT[7 * 128:8 * 128, t0:t0 + NT], reads=[zT_b], writes=[wl])
                P.dma("sync", al.ap, zT[8 * 128:9 * 128, t0:t0 + NT], reads=[zT_b], writes=[al])
                P.op("scalar", lambda e: e.activation(sgb.ap, gl.ap, AF.Sigmoid), reads=[gl], writes=[sgb])
                P.op("scalar", lambda e: e.activation(twb.ap, wl.ap, AF.Tanh), reads=[wl], writes=[twb])
                P.op("scalar", lambda e: e.activation(alb.ap, al.ap, AF.Identity), reads=[al], writes=[alb])
                for hp in range(2):
                    P.dma("sync", rT.ap, zT[(0 + hp) * 128:(1 + hp) * 128, t0:t0 + NT], reads=[zT_b], writes=[rT])
                    P.dma("sync", kT_.ap, zT[(2 + hp) * 128:(3 + hp) * 128, t0:t0 + NT], reads=[zT_b], writes=[kT_])
                    P.dma("sync", vT_.ap, zT[(4 + hp) * 128:(5 + hp) * 128, t0:t0 + NT], reads=[zT_b], writes=[vT_])
                    ps = nps()
                    P.op("tensor", lambda e, ps=ps, hp=hp: e.matmul(ps.ap, gup.ap[:, hp * 128:(hp + 1) * 128], sgb.ap,
                                                                    start=True, stop=True), reads=[gup, sgb], writes=[ps])
                    P.op("scalar", lambda e, ps=ps: e.activation(ev.ap, ps.ap, AF.Identity), reads=[ps], writes=[ev])
                    P.dma("sync", gTd[hp * 128:(hp + 1) * 128, t0:t0 + NT], ev.ap, reads=[ev], writes=[gTd_b])
                    P.op("vector", lambda e, hp=hp: e.tensor_scalar(kraw.ap, kT_.ap, rp.ap[:, 8 + hp:9 + hp], None, ALU.mult),
                         reads=[kT_, rp], writes=[kraw])
                    P.op("scalar", lambda e: e.activation(sqk.ap, kraw.ap, AF.Square), reads=[kraw], writes=[sqk])
                    ps = nps()
                    P.op("tensor", lambda e, ps=ps: e.matmul(ps.ap, blk.ap, sqk.ap, start=True, stop=True),
                         reads=[blk, sqk], writes=[ps])
                    P.op("scalar", lambda e, ps=ps: e.activation(tmp1.ap, ps.ap, AF.Sqrt), reads=[ps], writes=[tmp1])
                    P.op("vector", lambda e: e.tensor_scalar(tmp1.ap, tmp1.ap, 1e-12, None, ALU.max), reads=[tmp1], writes=[tmp1])
                    P.op("vector", lambda e: e.reciprocal(tmp1.ap, tmp1.ap), reads=[tmp1], writes=[tmp1])
                    P.op("vector", lambda e: e.tensor_tensor(kk.ap, kraw.ap, tmp1.ap, ALU.mult), reads=[kraw, tmp1], writes=[kk])
                    P.op("vector", lambda e, hp=hp: e.scalar_tensor_tensor(
                        tmp1.ap, rT.ap, rp.ap[:, 12 + hp:13 + hp], kT_.ap, ALU.mult, ALU.mult),
                        reads=[rT, rp, kT_], writes=[tmp1])
                    ps = nps()
                    P.op("tensor", lambda e, ps=ps: e.matmul(ps.ap, blk.ap, tmp1.ap, start=True, stop=True),
                         reads=[blk, tmp1], writes=[ps])
                    P.op("vector", lambda e, ps=ps: e.tensor_tensor(ev.ap, ps.ap, vT_.ap, ALU.mult), reads=[ps, vT_], writes=[ev])
                    P.dma("sync", bonT[hp * 128:(hp + 1) * 128, t0:t0 + NT], ev.ap, reads=[ev], writes=[bonT_b])
                    for d in range(2):
                        ps = nps()
                        P.op("tensor", lambda e, ps=ps, d=d, hp=hp: e.matmul(
                            ps.ap, wup.ap[d * 64:(d + 1) * 64, hp * 128:(hp + 1) * 128], twb.ap[d * 64:(d + 1) * 64, :],
                            start=True, stop=True), reads=[wup, twb], writes=[ps])
                        P.op("scalar", lambda e, ps=ps, d=d, hp=hp: e.activation(
                            lw.ap, ps.ap, AF.Sigmoid, bias=rp.ap[:, d * 2 + hp:d * 2 + hp + 1]), reads=[ps, rp], writes=[lw])
                        P.op("vector", lambda e: e.tensor_scalar(lw.ap, lw.ap, -float(np.exp(-0.5)), None, ALU.mult),
                             reads=[lw], writes=[lw])
                        ps = nps()
                        P.op("tensor", lambda e, ps=ps, d=d, hp=hp: e.matmul(
                            ps.ap, aup.ap[d * 64:(d + 1) * 64, hp * 128:(hp + 1) * 128], alb.ap[d * 64:(d + 1) * 64, :],
                            start=True, stop=True), reads=[aup, alb], writes=[ps])
                        P.op("scalar", lambda e, ps=ps, d=d, hp=hp: e.activation(
                            a_d.ap, ps.ap, AF.Sigmoid, bias=rp.ap[:, 4 + d * 2 + hp:5 + d * 2 + hp]), reads=[ps, rp], writes=[a_d])
                        P.op("vector", lambda e: e.tensor_tensor(bb.ap, kk.ap, a_d.ap, ALU.mult), reads=[kk, a_d], writes=[bb])
                        P.op("vector", lambda e, hp=hp: e.tensor_scalar(
                            tmp1.ap, a_d.ap, rp.ap[:, 10 + hp:11 + hp], omka.ap[:, hp:hp + 1], ALU.mult, ALU.add),
                            reads=[a_d, rp, omka], writes=[tmp1])
                        P.op("vector", lambda e: e.tensor_tensor(km.ap, kT_.ap, tmp1.ap, ALU.mult), reads=[kT_, tmp1], writes=[km])
                        U_ = Upre if d == 0 else Usuf
                        psl = nps()
                        for b4 in range(4):
                            pst = nps()
                            P.op("tensor", lambda e, pst=pst, b4=b4: e.matmul(
                                pst.ap[:, 0:128], lw.ap[:, b4 * 128:(b4 + 1) * 128], ident.ap, start=True, stop=True),
                                reads=[lw, ident], writes=[pst])
                            P.op("scalar", lambda e, pst=pst, b4=b4: e.activation(
                                lwt.ap[:, b4 * 128:(b4 + 1) * 128], pst.ap[:, 0:128], AF.Identity), reads=[pst], writes=[lwt])
                            P.op("tensor", lambda e, psl=psl, b4=b4, U_=U_: e.matmul(
                                psl.ap[:, b4 * 128:(b4 + 1) * 128], lwt.ap[:, b4 * 128:(b4 + 1) * 128], U_.ap,
                                start=True, stop=True), reads=[lwt, U_], writes=[psl])
                        P.op("scalar", lambda e, psl=psl: e.activation(LP.ap, psl.ap, AF.Identity), reads=[psl], writes=[LP])
                        P.op("scalar", lambda e: e.activation(eP.ap, LP.ap, AF.Exp), reads=[LP], writes=[eP])
                        P.op("scalar", lambda e: e.activation(eN.ap, LP.ap, AF.Exp, scale=-1.0), reads=[LP], writes=[eN])
                        P.op("vector", lambda e: e.tensor_tensor(LP.ap, LP.ap, lw.ap, ALU.subtract), reads=[LP, lw], writes=[LP])
                        P.op("scalar", lambda e: e.activation(ePm.ap, LP.ap, AF.Exp), reads=[LP], writes=[ePm])
                        pkv = pk.ap
                        P.op("vector", lambda e: e.tensor_tensor(pkv[:, :, 0:64], v3(kk), v3(ePm), ALU.mult),
                             reads=[kk, ePm], writes=[pk])
                        er = eP if d == 0 else ePm
                        P.op("vector", lambda e, er=er: e.tensor_tensor(pkv[:, :, 64:128], v3(rT), v3(er), ALU.mult),
                             reads=[rT, er], writes=[pk])
                        P.op("vector", lambda e: e.tensor_tensor(pkv[:, :, 128:192], v3(bb), v3(eN), ALU.mult),
                             reads=[bb, eN], writes=[pk])
                        P.op("vector", lambda e: e.tensor_tensor(pkv[:, :, 192:256], v3(km), v3(eN), ALU.mult),
                             reads=[km, eN], writes=[pk])
                        lc = (C - 1) if d == 0 else 0
                        P.op("vector", lambda e, lc=lc: e.tensor_copy(pkv[:, :, 384:385], v3(eP)[:, :, lc:lc + 1]),
                             reads=[eP], writes=[pk])
                        for c8 in range(8):
                            P.op("vector", lambda e, c8=c8: e.tensor_scalar(
                                pkv[:, c8, 256:320], pkv[:, c8, 128:192], pkv[:, c8, 384:385], None, ALU.mult),
                                reads=[pk], writes=[pk])
                            P.op("vector", lambda e, c8=c8: e.tensor_scalar(
                                pkv[:, c8, 320:384], pkv[:, c8, 192:256], pkv[:, c8, 384:385], None, ALU.mult),
                                reads=[pk], writes=[pk])
                        P.dma("sync", opsD[d, hp * 128:(hp + 1) * 128, tt * 8:(tt + 1) * 8, :], pkv,
                              reads=[pk], writes=[opsD_b])
            if cfg.get("rw_stage", 3) < 2:
                return
            P.barrier()
            a32.reset(kp32)
            b32 = Arena(a16.ap.bitcast(F32), NA16 // 2)
            b32.off = (k16 + 1) // 2
            msk = a32.alloc([5, 512], "rmask")
            build_nc.marks = {e: len(P.ops[e]) for e in ENGS}
            P.dma("sync", msk.ap[0:64], rmask.rearrange("p (a b) -> p a b", a=5), writes=[msk])
            mA, mBn, mB, mC, I8 = (msk.ap[0:64, i, :] for i in range(5))
            I64 = ident.ap[0:64, 0:64]
            NB_ = cfg.get("rw_nb", 2)
            OPb = [b32.alloc([8, 392], "OP%d" % i) for i in range(cfg.get("nb_op", NB_))]
            VTb = [b32.alloc([8, 64], "VT%d" % i) for i in range(cfg.get("nb_vt", NB_))]
            wn = ["PwA", "PwB", "PwTA", "PwTB", "XT", "LkT", "MbT", "MkT", "KapTok", "BCTok", "KCTok", "VTok",
                  "TK", "LkV", "nU2", "GT", "QmT", "STA", "STB"]
            wt = {}
            for i_, n in enumerate(wn):
                wt[n] = (a32 if i_ < 12 else b32).alloc([8, 64], n)
            Yo = [b32.alloc([8, 64], "Yo%d" % i) for i in range(cfg.get("nb_yo", NB_))]
            ST = [wt["STA"], wt["STB"]]
            P.op("vector", lambda e: e.memset(ST[0].ap, 0.0), writes=[ST[0]])

            def mm8(bank, lf, rf, start=True, stop=True, rd=()):
                for c in range(8):
                    P.op("tensor", lambda e, c=c: e.matmul(bank.ap[0:64, c * 64:(c + 1) * 64], lf(c), rf(c),
                                                           start=start, stop=stop), reads=list(rd), writes=[bank])

            def b3(bank):
                return bank.ap[0:64, :].rearrange("p (c t) -> p c t", t=64)
            for it in range(cfg.get("rw_iters", NCH)):
                cf, cb = it, NCH - 1 - it
                if cfg.get("rw_bar", True) and it > 0:
                    P.barrier()
                OP = OPb[it % len(OPb)]
                VT = VTb[it % len(VTb)]
                P.dma("sync", OP.ap[0:64, 0:4, :], opsD[0, :, cf, :].rearrange("(h k) x -> k h x", k=64),
                      reads=[opsD_b], writes=[OP])
                P.dma("sync", OP.ap[0:64, 4:8, :], opsD[1, :, cb, :].rearrange("(h k) x -> k h x", k=64),
                      reads=[opsD_b], writes=[OP])
                P.dma("sync", VT.ap[0:64, 0:4, :], zT[512:768, cf * C:(cf + 1) * C].rearrange("(h v) t -> v h t", v=64),
                      reads=[zT_b], writes=[VT])
                P.dma("sync", VT.ap[0:64, 4:8, :], zT[512:768, cb * C:(cb + 1) * C].rearrange("(h v) t -> v h t", v=64),
                      reads=[zT_b], writes=[VT])
                o = OP.ap
                KAP = lambda c, o=o: o[0:64, c, 0:64]
                RHO = lambda c, o=o: o[0:64, c, 64:128]
                BH = lambda c, o=o: o[0:64, c, 128:192]
                KH = lambda c, o=o: o[0:64, c, 192:256]
                BCf = lambda c, o=o: o[0:64, c, 256:320]
                KCf = lambda c, o=o: o[0:64, c, 320:384]
                W_ = lambda n: (lambda c: wt[n].ap[0:64, c, :])
                Ic = lambda c: I64
                mm8(psb[0], KAP, BH, rd=[OP])
                mm8(psb[1], BH, KAP, rd=[OP])
                mm8(psb[2], KH, KAP, rd=[OP])
                mm8(psb[3], BH, RHO, rd=[OP])
                mm8(psb[4], KH, RHO, rd=[OP])
                mm8(psb[5], KAP, Ic, rd=[OP, ident])
                mm8(psb[6], BCf, Ic, rd=[OP, ident])
                mm8(psb[7], KCf, Ic, rd=[OP, ident])
                Pw, PwT = wt["PwA"], wt["PwTA"]
                Pw2, PwT2 = wt["PwB"], wt["PwTB"]
                XT = wt["XT"]

                def ev_mask(dst, bank, m, eng="vector"):
                    P.op(eng, lambda e: e.tensor_tensor(dst.ap[0:64].rearrange("p c t -> p (c t)"), bank.ap[0:64, :], m, ALU.mult),
                         reads=[bank, msk], writes=[dst])

                def ev_copy(dst, bank, scale=None):
                    if scale is None:
                        P.op("scalar", lambda e: e.activation(dst.ap[0:64].rearrange("p c t -> p (c t)"), bank.ap[0:64, :],
                                                              AF.Identity), reads=[bank], writes=[dst])
                    else:
                        P.op("scalar", lambda e: e.activation(dst.ap[0:64].rearrange("p c t -> p (c t)"), bank.ap[0:64, :],
                                                              AF.Identity, scale=scale), reads=[bank], writes=[dst])
                ev_mask(Pw, psb[0], mA)
                ev_mask(PwT, psb[1], mBn)
                ev_mask(wt["LkT"], psb[2], mB)
                ev_mask(wt["MbT"], psb[3], mC)
                ev_mask(wt["MkT"], psb[4], mC)
                ev_copy(wt["KapTok"], psb[5])
                ev_copy(wt["BCTok"], psb[6])
                ev_copy(wt["KCTok"], psb[7])
                P.op("vector", lambda e, PwT=PwT: e.tensor_tensor(XT.ap[0:64].rearrange("p c t -> p (c t)"),
                                                                  PwT.ap[0:64].rearrange("p c t -> p (c t)"), I8, ALU.add),
                     reads=[PwT, msk], writes=[XT])
                mm8(psb[0], (lambda c, VT=VT: VT.ap[0:64, c, :]), Ic, rd=[VT, ident])
                ev_copy(wt["VTok"], psb[0])
                for r_ in range(5):
                    mm8(psb[1], (lambda c, PwT=PwT: PwT.ap[0:64, c, :]), (lambda c, Pw=Pw: Pw.ap[0:64, c, :]), rd=[Pw, PwT])
                    if r_ < 4:
                        mm8(psb[2], (lambda c, Pw=Pw: Pw.ap[0:64, c, :]), (lambda c, PwT=PwT: PwT.ap[0:64, c, :]), rd=[Pw, PwT])
                    ev_copy(Pw2, psb[1])
                    if r_ < 4:
                        P.op("vector", lambda e, PwT2=PwT2: e.tensor_copy(PwT2.ap[0:64].rearrange("p c t -> p (c t)"),
                                                                           psb[2].ap[0:64, :]), reads=[psb[2]], writes=[PwT2])
                    mm8(psb[3], (lambda c, Pw2=Pw2: Pw2.ap[0:64, c, :]), (lambda c: XT.ap[0:64, c, :]), rd=[Pw2, XT])
                    P.op("vector", lambda e: e.tensor_tensor(XT.ap[0:64].rearrange("p c t -> p (c t)"),
                                                             XT.ap[0:64].rearrange("p c t -> p (c t)"),
                                                             psb[3].ap[0:64, :], ALU.add), reads=[XT, psb[3]], writes=[XT])
                    Pw, Pw2 = Pw2, Pw
                    PwT, PwT2 = PwT2, PwT
                XTf = lambda c: XT.ap[0:64, c, :]
                mm8(psb[4], XTf, W_("KapTok"), rd=[XT, wt["KapTok"]])
                ev_copy(wt["TK"], psb[4])
                mm8(psb[5], W_("LkT"), W_("VTok"), rd=[wt["LkT"], wt["VTok"]])
                P.op("vector", lambda e: e.tensor_copy(wt["LkV"].ap[0:64].rearrange("p c t -> p (c t)"), psb[5].ap[0:64, :]),
                     reads=[psb[5]], writes=[wt["LkV"]])
                mm8(psb[6], XTf, W_("LkV"), rd=[XT, wt["LkV"]])
                ev_copy(wt["nU2"], psb[6], scale=-1.0)
                mm8(psb[7], W_("TK"), W_("BCTok"), rd=[wt["TK"], wt["BCTok"]])
                for c in range(8):
                    P.op("vector", lambda e, c=c, o=o: e.scalar_tensor_tensor(
                        wt["GT"].ap[0:64, c, :], I64, o[0:64, c, 384:385], psb[7].ap[0:64, c * 64:(c + 1) * 64],
                        ALU.mult, ALU.subtract), reads=[ident, OP, psb[7]], writes=[wt["GT"]])
                mm8(psb[1], W_("TK"), W_("MbT"), rd=[wt["TK"], wt["MbT"]])
                P.op("vector", lambda e, o=o: e.tensor_tensor(wt["QmT"].ap[0:64], o[0:64, :, 64:128], b3(psb[1]), ALU.subtract),
                     reads=[OP, psb[1]], writes=[wt["QmT"]])
                S0 = ST[it % 2]
                S1 = ST[(it + 1) % 2]
                for c in range(8):
                    trip = [("VTok", "MkT"), ("nU2", "MbT"), (None, "QmT")]
                    for gi, (ln, rn) in enumerate(trip):
                        lsrc = S0 if ln is None else wt[ln]
                        P.op("tensor", lambda e, c=c, lsrc=lsrc, rn=rn, gi=gi: e.matmul(
                            psb[2].ap[0:64, c * 64:(c + 1) * 64], lsrc.ap[0:64, c, :], wt[rn].ap[0:64, c, :],
                            start=(gi == 0), stop=(gi == 2)), reads=[lsrc, wt[rn]], writes=[psb[2]])
                Y_ = Yo[it % len(Yo)]
                ev_copy(Y_, psb[2])
                P.dma("sync", yD[0, :, cf * C:(cf + 1) * C].rearrange("(h v) t -> v h t", v=64), Y_.ap[0:64, 0:4, :],
                      reads=[Y_], writes=[yD_b])
                P.dma("sync", yD[1, :, cb * C:(cb + 1) * C].rearrange("(h v) t -> v h t", v=64), Y_.ap[0:64, 4:8, :],
                      reads=[Y_], writes=[yD_b])
                for c in range(8):
                    trip = [("KCTok", "VTok"), ("BCTok", "nU2"), ("GT", None)]
                    for gi, (ln, rn) in enumerate(trip):
                        rsrc = S0 if rn is None else wt[rn]
                        P.op("tensor", lambda e, c=c, ln=ln, rsrc=rsrc, gi=gi: e.matmul(
                            psb[0].ap[0:64, c * 64:(c + 1) * 64], wt[ln].ap[0:64, c, :], rsrc.ap[0:64, c, :],
                            start=(gi == 0), stop=(gi == 2)), reads=[wt[ln], rsrc], writes=[psb[0]])
                P.op("vector", lambda e, S1=S1: e.tensor_copy(S1.ap[0:64].rearrange("p c t -> p (c t)"), psb[0].ap[0:64, :]),
                     reads=[psb[0]], writes=[S1])
                if cfg.get("debug") and it == 0:
                    for di_, n_ in enumerate(wn):
                        P.dma("sync", dbgT[di_], wt[n_].ap[0:64].rearrange("p c t -> p (c t)"), reads=[wt[n_]], writes=[dbgT_b])
                    P.dma("sync", dbgT[20], Y_.ap[0:64].rearrange("p c t -> p (c t)"), reads=[Y_], writes=[dbgT_b])
                    P.dma("sync", dbgT[21], OP.ap[0:64, :, 0:64], reads=[OP], writes=[dbgT_b])
                    P.dma("sync", dbgT[22], VT.ap[0:64].rearrange("p c t -> p (c t)"), reads=[VT], writes=[dbgT_b])
                    P.dma("sync", dbgT[23], msk.ap[0:64, 0, :], reads=[msk], writes=[dbgT_b])
            if cfg.get("rw_stage", 3) < 3:
                return
            P.barrier()
            a32.reset(kp32)
            a16.off = k16
            yf = [a32.alloc([NT], "yf%d" % i) for i in range(2)]
            yb = [a32.alloc([NT], "yb%d" % i) for i in range(2)]
            bo = [a32.alloc([NT], "bo%d" % i) for i in range(2)]
            gg = [a32.alloc([NT], "gg%d" % i) for i in range(2)]
            dd = a32.alloc([NT], "dd")
            sqd = a32.alloc([NT], "sqd")
            rsd = a32.alloc([NT], "rsd")
            ob = [a16.alloc([NT], "ob%d" % i) for i in range(2)]
            for tt in range(T // NT):
                t0 = tt * NT
                for hp in range(2):
                    i2 = (tt * 2 + hp) % 2
                    rows = slice(hp * 128, (hp + 1) * 128)
                    P.dma("sync", yf[i2].ap, yD[0, rows, t0:t0 + NT], reads=[yD_b], writes=[yf[i2]])
                    P.dma("sync", yb[i2].ap, yD[1, rows, t0:t0 + NT], reads=[yD_b], writes=[yb[i2]])
                    P.dma("sync", bo[i2].ap, bonT[rows, t0:t0 + NT], reads=[bonT_b], writes=[bo[i2]])
                    P.dma("sync", gg[i2].ap, gTd[rows, t0:t0 + NT], reads=[gTd_b], writes=[gg[i2]])
                    y_ = yf[i2]
                    P.op("vector", lambda e, y_=y_, i2=i2: e.tensor_tensor(y_.ap, y_.ap, yb[i2].ap, ALU.add),
                         reads=[y_, yb[i2]], writes=[y_])
                    ps = nps()
                    P.op("tensor", lambda e, ps=ps, y_=y_: e.matmul(ps.ap, blk64.ap, y_.ap, start=True, stop=True),
                         reads=[blk64, y_], writes=[ps])
                    P.op("vector", lambda e, ps=ps, y_=y_: e.tensor_tensor(dd.ap, y_.ap, ps.ap, ALU.subtract),
                         reads=[y_, ps], writes=[dd])
                    P.op("scalar", lambda e: e.activation(sqd.ap, dd.ap, AF.Square), reads=[dd], writes=[sqd])
                    ps = nps()
                    P.op("tensor", lambda e, ps=ps: e.matmul(ps.ap, blk64.ap, sqd.ap, start=True, stop=True),
                         reads=[blk64, sqd], writes=[ps])
                    P.op("scalar", lambda e, ps=ps: e.activation(rsd.ap, ps.ap, AF.Sqrt, bias=c12.ap[:, 0:1], scale=1.0),
                         reads=[ps, c12], writes=[rsd])
                    P.op("vector", lambda e: e.reciprocal(rsd.ap, rsd.ap), reads=[rsd], writes=[rsd])
                    P.op("vector", lambda e: e.tensor_tensor(dd.ap, dd.ap, rsd.ap, ALU.mult), reads=[dd, rsd], writes=[dd])
                    P.op("vector", lambda e, hp=hp: e.tensor_scalar(
                        dd.ap, dd.ap, rp.ap[:, 14 + hp:15 + hp], rp.ap[:, 16 + hp:17 + hp], ALU.mult, ALU.add),
                        reads=[dd, rp], writes=[dd])
                    P.op("vector", lambda e, i2=i2: e.tensor_tensor(dd.ap, dd.ap, bo[i2].ap, ALU.add),
                         reads=[dd, bo[i2]], writes=[dd])
                    o_ = ob[i2]
                    P.op("vector", lambda e, i2=i2, o_=o_: e.tensor_tensor(o_.ap, dd.ap, gg[i2].ap, ALU.mult),
                         reads=[dd, gg[i2]], writes=[o_])
                    P.dma("sync", oT[rows, t0:t0 + NT], o_.ap, reads=[o_], writes=[oT_b])

        def phase_out(l):
            P.barrier()
            a32.reset(keep32)
            a16.reset(keep16)
            wo = a16.alloc([8, D], "wo")
            ot = [a16.alloc([8, NT], "ot%d" % i) for i in range(2)]
            xr = [a32.alloc([NT], "xr%d" % i) for i in range(3)]
            P.dma("gpsimd", wo.ap, w_out[l].rearrange("(kc p) n -> p kc n", p=128), writes=[wo])
            for tt in range(T // NT):
                t0 = tt * NT
                g = t0 // G
                o_ = ot[tt % 2]
                P.dma("sync", o_.ap, oT[:, t0:t0 + NT].rearrange("(kc p) t -> p kc t", p=128),
                      reads=[oT_b], writes=[o_])
                for dmc in range(8):
                    po = nps()
                    for kc in range(8):
                        P.op("tensor", lambda e, kc=kc, po=po, o_=o_, dmc=dmc: e.matmul(
                            po.ap, wo.ap[:, kc, dmc * 128:(dmc + 1) * 128], o_.ap[:, kc, :],
                            start=(kc == 0), stop=(kc == 7)), reads=[wo, o_], writes=[po])
                    x_ = xr[dmc % 3]
                    P.dma("sync", x_.ap, xs[dmc * 128:(dmc + 1) * 128, t0:t0 + NT], reads=[xs_g[g]], writes=[x_])
                    P.op("vector", lambda e, po=po, x_=x_, dmc=dmc: e.scalar_tensor_tensor(
                        x_.ap, po.ap, hg.ap[:, 8 + dmc:8 + dmc + 1], x_.ap, ALU.mult, ALU.add),
                        reads=[po, hg, x_], writes=[x_])
                    P.dma("sync", xs[dmc * 128:(dmc + 1) * 128, t0:t0 + NT], x_.ap, reads=[x_], writes=[xs_g[g]])

        def phase_zero_o():
            P.barrier()
            a16.reset(keep16)
            zt = a16.alloc([8, NT], "zt")
            P.op("vector", lambda e: e.memset(zt.ap, 0.0), writes=[zt])
            for tt in range(T // NT):
                P.dma("sync", oT[:, tt * NT:(tt + 1) * NT].rearrange("(kc p) t -> p kc t", p=128), zt.ap,
                      reads=[zt], writes=[oT_b])

        cur, cur_g = xT, xT_g
        if cfg.get("mix", True):
            phase_tables()
            phase_zero_o()
        for l in range(nl):
            phase_mod(l)
            if cfg.get("ffn1", True):
                phase_ffn(l, 0, cur, cur_g, xs, xs_g)
                cur, cur_g = xs, xs_g
            if cfg.get("mix", True):
                phase_z(l)
                if cfg.get("mla", True):
                    phase_mla(l)
                if cfg.get("ret", True):
                    phase_ret(l)
                if cfg.get("rwkv", True):
                    phase_rwkv(l)
                phase_out(l)
            if cfg.get("ffn2", True):
                phase_ffn(l, 1, cur, cur_g, xs, xs_g)
                cur, cur_g = xs, xs_g
        phase_final(cur, cur_g)
        P.barrier()
        P.emit(st)
        build_nc.ninstr = P.n
        build_nc.P = P
    return nc


_CACHE = {}


def _rwp(inputs):
    g = lambda n: np.asarray(inputs[n], dtype=np.float32)
    out = np.zeros((DEPTH, 128, 18), np.float32)
    w0 = g("rwkv_w0"); a0 = g("rwkv_a0")
    for d in range(2):
        for hp in range(2):
            out[:, :, d * 2 + hp] = w0[:, d, hp * 128:(hp + 1) * 128]
            out[:, :, 4 + d * 2 + hp] = a0[:, d, hp * 128:(hp + 1) * 128]
    for j, n in enumerate(["rwkv_k_k", "rwkv_k_a", "rwkv_r_k", "rwkv_ln_g", "rwkv_ln_b"]):
        a = g(n).reshape(DEPTH, 256)
        for hp in range(2):
            out[:, :, 8 + 2 * j + hp] = a[:, hp * 128:(hp + 1) * 128]
    return out


def _rmask():
    i = np.arange(64)
    lt = (i[None, :] < i[:, None]).astype(np.float32)
    gt = (i[None, :] > i[:, None]).astype(np.float32)
    le = (i[:, None] <= i[None, :]).astype(np.float32)
    m = np.zeros((64, 5, 8, 64), np.float32)
    for c in range(8):
        f = c < 4
        m[:, 0, c, :] = -(lt if f else gt)
        m[:, 1, c, :] = -(gt if f else lt)
        m[:, 2, c, :] = (gt if f else lt)
        m[:, 3, c, :] = (le if f else lt)
        m[:, 4, c, :] = np.eye(64, dtype=np.float32)
    return np.ascontiguousarray(m.reshape(64, 2560))


def _ucst():
    i = np.arange(128)
    same = (i[:, None] // 64) == (i[None, :] // 64)
    u = np.zeros((128, 384), np.float32)
    u[:, 0:128] = (same & (i[:, None] <= i[None, :]))
    u[:, 128:256] = (same & (i[:, None] >= i[None, :]))
    u[:, 256:384] = same
    return u


def _basec():
    b = np.zeros((128, 576), np.float32)
    b[:, 0:512] = np.arange(512)[None, :] - np.arange(128)[:, None]
    b[:, 512:572] = (np.arange(60) * 128 - 3968)[None, :]
    return b


def _consts():
    cst = np.zeros((128, 136), np.float32)
    cst[:, 0:128] = np.eye(128, dtype=np.float32)
    p = np.arange(128)
    cst[:, 128] = (10000.0 ** (-(2.0 * (p % 32)) / 64.0)).astype(np.float32)
    cst[:, 129] = np.where((p % 64) < 32, -1.0, 1.0)
    pm = (p - 64) % 16
    cst[:, 130] = (10000.0 ** (-(2.0 * pm) / 32.0)).astype(np.float32)
    cst[:, 131] = np.where(((p - 64) % 32) < 16, -1.0, 1.0)
    return cst


def kernel(**inputs):
    cfg = inputs.pop("_cfg", {})
    ncores = cfg.get("ncores", 4)
    key = repr(sorted(cfg.items()))
    if key not in _CACHE:
        _CACHE[key] = build_nc(cfg)
    nc = _CACHE[key]
    f = lambda a: np.ascontiguousarray(np.asarray(a, dtype=np.float32))
    x = f(inputs["x"])
    c = f(inputs["c"])
    shared = {
        "w_ada": f(inputs["w_ada"]),
        "b_adaT": np.ascontiguousarray(f(inputs["b_ada"]).reshape(DEPTH, 72, 128).transpose(0, 2, 1)),
        "w_ff1_in": f(inputs["w_ff1_in"]), "w_ff2_in": f(inputs["w_ff2_in"]),
        "w_ff1_out": f(inputs["w_ff1_out"]), "w_ff2_out": f(inputs["w_ff2_out"]),
        "fng": np.ascontiguousarray(f(inputs["final_norm_g"]).reshape(8, 128).T),
        "w_in": f(inputs["w_in"]), "w_out": f(inputs["w_out"]), "rwkv_mu": f(inputs["rwkv_mu"]),
        "mla_qgT": np.ascontiguousarray(f(inputs["mla_q_norm_g"]).reshape(DEPTH, 3, 128).transpose(0, 2, 1)),
        "mla_kvgT": np.ascontiguousarray(f(inputs["mla_kv_norm_g"]).reshape(DEPTH, 2, 128).transpose(0, 2, 1)),
        "mla_w_uq": f(inputs["mla_w_uq"]), "mla_w_ukv": f(inputs["mla_w_ukv"]),
        "cst": _consts(), "basec": _basec(), "rmask": _rmask(), "ucst": _ucst(),
        "rwp": _rwp(inputs),
        "rw_wup": np.ascontiguousarray(f(inputs["rwkv_w_up"]).reshape(DEPTH, 128, 256)),
        "rw_aup": np.ascontiguousarray(f(inputs["rwkv_a_up"]).reshape(DEPTH, 128, 256)),
        "rw_gup": f(inputs["rwkv_g_up"]),
        "ret_lr": np.ascontiguousarray(f(inputs["ret_log_rate"]).reshape(DEPTH, 1, 8)),
        "ret_gT": np.ascontiguousarray(f(inputs["ret_gn_g"]).transpose(0, 2, 1)),
    }
    positions = np.asarray(inputs["positions"]).astype(np.int32)
    in_maps = []
    for i in range(ncores):
        b = i % 4
        m = dict(shared)
        m["xT"] = np.ascontiguousarray(x[b].T)
        m["cvec"] = np.ascontiguousarray(c[b].reshape(8, 128).T)
        m["pos"] = np.ascontiguousarray(positions[b:b + 1])
        in_maps.append(m)
    res = run_bass_kernel_spmd(nc, in_maps, core_ids=list(range(ncores)))
    if cfg.get("debug"):
        kernel.dbg = {k: res.results[0][k] for k in ("dbgT", "yD", "gTd", "bonT")}
    out = np.stack([np.ascontiguousarray(res.results[b % ncores]["yT"].T) for b in range(4)], axis=0)
    return out.astype(np.float32)
```
